# Optimizing a Trainium2 kernel written in Bass

```python
import jax, jax.numpy as jnp
from jax import lax
import numpy as np

D_MODEL = 1024
BATCH = 8
SEQ = 2048
DEPTH = 4

HEAD_DIM = 64
N_HEADS = D_MODEL // HEAD_DIM
INNER = N_HEADS * HEAD_DIM
ROT_DIM = HEAD_DIM // 4
ROPE_THETA = 500000.0
DSWA_GROUPS = ((128, 1), (512, 4), (2048, 16))
N_GROUPS = len(DSWA_GROUPS)
BLOCK = 128
N_MIXERS = 2
EPS = 1e-6
FOX_IN = 4 * INNER + N_HEADS
DSW_IN = 3 * N_GROUPS * INNER + INNER

kernel_name = "fox_dilated_swa_gated_hybrid"


def rms_norm(x, g):
    xf = x.astype(jnp.float32)
    y = xf * lax.rsqrt(jnp.mean(xf * xf, axis=-1, keepdims=True) + EPS) * g.astype(jnp.float32)
    return y.astype(x.dtype)


def rotary_tables(positions):
    inv_freq = ROPE_THETA ** (-jnp.arange(0, ROT_DIM, 2, dtype=jnp.float32) / ROT_DIM)
    ang = positions.astype(jnp.float32)[..., None] * inv_freq
    return jnp.cos(ang), jnp.sin(ang)


def apply_partial_rotary(t, cos, sin):
    half = ROT_DIM // 2
    c = cos[:, :, None, None, :].astype(t.dtype)
    s = sin[:, :, None, None, :].astype(t.dtype)
    t1, t2, rest = t[..., :half], t[..., half:ROT_DIM], t[..., ROT_DIM:]
    return jnp.concatenate([t1 * c - t2 * s, t2 * c + t1 * s, rest], axis=-1)


def fox_attention(q, k, v, c):
    B, S, H, dh = q.shape
    nq = S // BLOCK
    scale = dh ** -0.5
    qb = q.reshape(B, nq, BLOCK, H, dh).transpose(1, 0, 2, 3, 4)
    cb = c.reshape(B, nq, BLOCK, H).transpose(1, 0, 3, 2)
    c_k = c.transpose(0, 2, 1)
    key_pos = jnp.arange(S)

    def one_block(args):
        qi, ci, bi = args
        s = jnp.einsum('bqhd,bkhd->bhqk', qi, k).astype(jnp.float32) * scale
        s = s + ci[..., None] - c_k[:, :, None, :]
        q_pos = bi * BLOCK + jnp.arange(BLOCK)
        mask = key_pos[None, :] <= q_pos[:, None]
        s = jnp.where(mask, s, -jnp.inf)
        p = jax.nn.softmax(s, axis=-1).astype(v.dtype)
        return jnp.einsum('bhqk,bkhd->bqhd', p, v)

    o = lax.map(one_block, (qb, cb, jnp.arange(nq)))
    return o.transpose(1, 0, 2, 3, 4).reshape(B, S, H, dh)


def fox_mixer(h, w_in, b_f, q_g, k_g):
    B, S, _ = h.shape
    proj = h @ w_in
    q = rms_norm(proj[..., :INNER].reshape(B, S, N_HEADS, HEAD_DIM), q_g)
    k = rms_norm(proj[..., INNER:2 * INNER].reshape(B, S, N_HEADS, HEAD_DIM), k_g)
    v = proj[..., 2 * INNER:3 * INNER].reshape(B, S, N_HEADS, HEAD_DIM)
    gate = proj[..., 3 * INNER:4 * INNER]
    f_logit = proj[..., 4 * INNER:].astype(jnp.float32) + b_f.astype(jnp.float32)
    c = jnp.cumsum(jax.nn.log_sigmoid(f_logit), axis=1)
    o = fox_attention(q, k, v, c)
    return o.reshape(B, S, INNER), gate


def _band(t):
    prev = jnp.concatenate([jnp.zeros_like(t[:, :1]), t[:, :-1]], axis=1)
    return jnp.concatenate([prev, t], axis=2)


def dilated_window_attention(q, k, v, window, dilation):
    B, S, H, dh = q.shape
    r = dilation
    L = S // r
    n_back = window // r
    nb = -(-L // BLOCK)
    Lp = nb * BLOCK
    scale = dh ** -0.5

    def to_blocks(t):
        t = t.reshape(B, L, r, H, dh).transpose(0, 2, 1, 3, 4).reshape(B * r, L, H, dh)
        t = jnp.pad(t, ((0, 0), (0, Lp - L), (0, 0), (0, 0)))
        return t.reshape(B * r, nb, BLOCK, H, dh)

    qb, kb, vb = to_blocks(q), to_blocks(k), to_blocks(v)
    s = jnp.einsum('znqhd,znkhd->znhqk', qb, _band(kb)).astype(jnp.float32) * scale
    q_idx = jnp.arange(nb)[:, None] * BLOCK + jnp.arange(BLOCK)[None, :]
    k_idx = jnp.arange(nb)[:, None] * BLOCK - BLOCK + jnp.arange(2 * BLOCK)[None, :]
    dist = q_idx[:, :, None] - k_idx[:, None, :]
    mask = (dist >= 0) & (dist <= n_back) & (k_idx[:, None, :] >= 0)
    s = jnp.where(mask[None, :, None], s, -jnp.inf)
    lse = jax.nn.logsumexp(s, axis=-1)
    p = jnp.exp(s - lse[..., None]).astype(v.dtype)
    o = jnp.einsum('znhqk,znkhd->znqhd', p, _band(vb))
    o = o.reshape(B * r, Lp, H, dh)[:, :L]
    o = o.reshape(B, r, L, H, dh).transpose(0, 2, 1, 3, 4).reshape(B, S, H, dh)
    lse = lse.transpose(0, 1, 3, 2).reshape(B * r, Lp, H)[:, :L]
    lse = lse.reshape(B, r, L, H).transpose(0, 2, 1, 3).reshape(B, S, H)
    return o, lse


def dsw_mixer(h, cos, sin, w_in, q_g, k_g):
    B, S, _ = h.shape
    proj = h @ w_in
    qkv = proj[..., :3 * N_GROUPS * INNER].reshape(B, S, 3, N_GROUPS, N_HEADS, HEAD_DIM)
    gate = proj[..., 3 * N_GROUPS * INNER:]
    q = apply_partial_rotary(rms_norm(qkv[:, :, 0], q_g[:, None, :]), cos, sin)
    k = apply_partial_rotary(rms_norm(qkv[:, :, 1], k_g[:, None, :]), cos, sin)
    v = qkv[:, :, 2]
    outs, lses = [], []
    for g, (window, dilation) in enumerate(DSWA_GROUPS):
        o_g, lse_g = dilated_window_attention(q[:, :, g], k[:, :, g], v[:, :, g], window, dilation)
        outs.append(o_g)
        lses.append(lse_g)
    wts = jax.nn.softmax(jnp.stack(lses, axis=0), axis=0).astype(v.dtype)
    o = jnp.einsum('gbsh,gbshd->bshd', wts, jnp.stack(outs, axis=0))
    return o.reshape(B, S, INNER), gate


def setup_inputs(seed: int = 0) -> dict:
    key = jax.random.key(seed)
    ks = jax.random.split(key, 13)
    n_fox = (DEPTH + 1) // N_MIXERS
    n_dsw = DEPTH // N_MIXERS
    f32 = jnp.float32
    x = jax.random.normal(ks[0], (BATCH, SEQ, D_MODEL), f32)
    offset = jax.random.randint(ks[1], (BATCH, 1), 0, 4096, dtype=jnp.int32)
    positions = offset + jnp.arange(SEQ, dtype=jnp.int32)[None, :]
    norm_g = 1.0 + 0.02 * jax.random.normal(ks[2], (DEPTH, D_MODEL), f32)
    fox_w_in = jax.random.normal(ks[3], (n_fox, D_MODEL, FOX_IN), f32) * D_MODEL ** -0.5
    fox_b_f = jax.random.uniform(ks[4], (n_fox, N_HEADS), f32, minval=1.0, maxval=4.0)
    fox_q_norm = 1.0 + 0.02 * jax.random.normal(ks[5], (n_fox, HEAD_DIM), f32)
    fox_k_norm = 1.0 + 0.02 * jax.random.normal(ks[6], (n_fox, HEAD_DIM), f32)
    fox_w_out = jax.random.normal(ks[7], (n_fox, INNER, D_MODEL), f32) * (2 * INNER) ** -0.5
    dsw_w_in = jax.random.normal(ks[8], (n_dsw, D_MODEL, DSW_IN), f32) * D_MODEL ** -0.5
    dsw_q_norm = 1.0 + 0.02 * jax.random.normal(ks[9], (n_dsw, N_GROUPS, HEAD_DIM), f32)
    dsw_k_norm = 1.0 + 0.02 * jax.random.normal(ks[10], (n_dsw, N_GROUPS, HEAD_DIM), f32)
    dsw_w_out = jax.random.normal(ks[11], (n_dsw, INNER, D_MODEL), f32) * (2 * INNER) ** -0.5
    return {"x": x, "positions": positions, "norm_g": norm_g,
            "fox_w_in": fox_w_in, "fox_b_f": fox_b_f, "fox_q_norm": fox_q_norm,
            "fox_k_norm": fox_k_norm, "fox_w_out": fox_w_out,
            "dsw_w_in": dsw_w_in, "dsw_q_norm": dsw_q_norm, "dsw_k_norm": dsw_k_norm,
            "dsw_w_out": dsw_w_out}


def reference(x, positions, norm_g, fox_w_in, fox_b_f, fox_q_norm, fox_k_norm, fox_w_out,
              dsw_w_in, dsw_q_norm, dsw_k_norm, dsw_w_out):
    cos, sin = rotary_tables(positions)
    for i in range(DEPTH):
        j = i // N_MIXERS
        h = rms_norm(x, norm_g[i])
        if i % N_MIXERS == 0:
            o, gate = fox_mixer(h, fox_w_in[j], fox_b_f[j], fox_q_norm[j], fox_k_norm[j])
            w_out = fox_w_out[j]
        else:
            o, gate = dsw_mixer(h, cos, sin, dsw_w_in[j], dsw_q_norm[j], dsw_k_norm[j])
            w_out = dsw_w_out[j]
        x = x + (o * jax.nn.silu(gate)) @ w_out
    return x
```

```python
import numpy as np
from contextlib import ExitStack
import concourse.bass as bass
import concourse.mybir as mybir
from concourse.bass_utils import run_bass_kernel_spmd

F32 = mybir.dt.float32
BF16 = mybir.dt.bfloat16
I32 = mybir.dt.int32
AF = mybir.ActivationFunctionType
ALU = mybir.AluOpType

D = 1024
S = 2048
NH = 16
HD = 64
DEPTH = 4
EPS = 1e-6
NEG = -30000.0
ENGS = ("pe", "act", "dve", "pool", "sp")
LAUNCH_GROUPS = [[0, 1, 2, 3]]
DSWA = ((128, 1), (512, 4), (2048, 16))
PW = 16


class Res:
    __slots__ = ("name", "w", "r", "excl", "strict")

    def __init__(self, name, excl=False, strict=False):
        self.name = name
        self.w = None
        self.r = {}
        self.excl = excl
        self.strict = strict


class Sched:
    def __init__(self):
        self.ops = {e: [] for e in ENGS}
        self.known = {e: {} for e in ENGS}
        self.dcount = {}
        self.dsnap = {}

    def _collect(self, eng, reads, writes):
        deps = {}

        def add(tok, raw):
            if tok is None:
                return
            kind, key, idx = tok
            if kind == "e" and key == eng and (eng == "pe" or not raw):
                return
            k = (kind, key)
            if deps.get(k, -1) < idx:
                deps[k] = idx

        for r in reads:
            add(r.w, True)
            if r.excl:
                for k, idx in r.r.items():
                    if not (k[0] == "e" and k[1] == eng):
                        add((k[0], k[1], idx), False)
        for w in writes:
            add(w.w, w.strict)
            for k, idx in w.r.items():
                add((k[0], k[1], idx), w.strict)
        kn = self.known[eng]
        waits = []
        for k, idx in deps.items():
            if kn.get(k, -1) >= idx:
                continue
            waits.append((k, idx))
        for k, idx in waits:
            if kn.get(k, -1) < idx:
                kn[k] = idx
            snap = self.ops[k[1]][idx]["snap"] if k[0] == "e" else self.dsnap.get((k[1], idx), {})
            for kk, vv in snap.items():
                if kn.get(kk, -1) < vv:
                    kn[kk] = vv
            if k[0] == "e":
                self.ops[k[1]][idx]["sig"] = True
        return waits

    def op(self, eng, fn, reads=(), writes=()):
        waits = self._collect(eng, reads, writes)
        idx = len(self.ops[eng])
        self.ops[eng].append({"kind": "c", "fn": fn, "waits": waits, "sig": False,
                              "snap": dict(self.known[eng])})
        tok = ("e", eng)
        for r in reads:
            r.r[tok] = idx
        for w in writes:
            w.w = ("e", eng, idx)
            w.r = {}

    def dma(self, eng, key, fn, reads=(), writes=()):
        waits = self._collect(eng, reads, writes)
        cnt = self.dcount.get(key, 0) + 1
        self.dcount[key] = cnt
        self.ops[eng].append({"kind": "d", "fn": fn, "waits": waits, "sig": False, "dkey": key,
                              "snap": dict(self.known[eng])})
        self.dsnap[(key, cnt)] = dict(self.known[eng])
        tok = ("d", key)
        for r in reads:
            r.r[tok] = cnt
        for w in writes:
            w.w = ("d", key, cnt)
            w.r = {}

    def wait_all(self, eng, reslist):
        waits = self._collect(eng, reslist, ())
        self.ops[eng].append({"kind": "w", "fn": None, "waits": waits, "sig": False,
                              "snap": dict(self.known[eng])})

    def emit(self, nc, es):
        sems = {e: es.enter_context(nc.semaphore("s_" + e)) for e in ENGS}
        dsems = {k: es.enter_context(nc.semaphore("d_" + k)) for k in self.dcount}
        for e in ENGS:
            c = 0
            for o in self.ops[e]:
                if o["kind"] == "c" and o["sig"]:
                    c += 1
                    o["val"] = c
        ops = self.ops

        def run(e, eng):
            for o in ops[e]:
                wl = [(sems[k[1]], ops[k[1]][idx]["val"]) if k[0] == "e" else (dsems[k[1]], 16 * idx)
                      for k, idx in o["waits"]]
                attach = None
                if o["kind"] != "w" and wl:
                    attach = wl.pop()
                for sm_, v_ in wl:
                    eng.wait_ge(sm_, v_)
                if o["kind"] == "w":
                    continue
                ins = o["fn"](eng)
                if attach is not None:
                    ins._wait_ge(attach[0], attach[1])
                if o["kind"] == "d":
                    ins.then_inc(dsems[o["dkey"]], 16)
                elif o["sig"]:
                    ins.then_inc(sems[e], 1)

        with nc.Block() as block:
            @block.tensor
            def _(t):
                run("pe", t)

            @block.scalar
            def _(a):
                run("act", a)

            @block.vector
            def _(v):
                run("dve", v)

            @block.gpsimd
            def _(g):
                run("pool", g)

            @block.sync
            def _(s):
                run("sp", s)


def run_pipeline(jobs):
    n = len(jobs)
    if n == 0:
        return
    mx = max(len(j) for j in jobs)
    for t in range(n + mx - 1):
        for k in range(mx):
            i = t - k
            if 0 <= i < n and k < len(jobs[i]):
                jobs[i][k]()


def build_program(layers):
    nc = bass.Bass("TRN2", target_bir_lowering=False)
    es = ExitStack()
    sc = Sched()

    x_d = nc.dram_tensor("x", [S, D], F32, kind="ExternalInput").ap()
    y_d = nc.dram_tensor("y", [S, D], F32, kind="ExternalOutput").ap()
    pos_d = nc.dram_tensor("pos", [128, S], I32, kind="ExternalInput").ap()
    par_d = nc.dram_tensor("par", [128, DEPTH * PW], F32, kind="ExternalInput").ap()
    cpar_d = nc.dram_tensor("cpar", [128, 4], F32, kind="ExternalInput").ap()
    cmat_d = nc.dram_tensor("cmat", [128, 6 * 128 + 16 * 70], F32, kind="ExternalInput").ap()
    wfox_d = nc.dram_tensor("wfox", [2, 8, 128, 4096], F32, kind="ExternalInput").ap()
    wff_d = nc.dram_tensor("wff", [2, 128, 8 * 96], F32, kind="ExternalInput").ap()
    wdsw_d = nc.dram_tensor("wdsw", [2, 24, 128, 4096], F32, kind="ExternalInput").ap()
    wout_d = nc.dram_tensor("wout", [4, 2, 128, 4096], F32, kind="ExternalInput").ap()

    def sb(name, shape, dt):
        return es.enter_context(nc.sbuf_tensor(name, shape, dt))

    xres = sb("xres", [128, 16, D], F32)
    hT = sb("hT", [128, 8, S], BF16)
    ogT = sb("ogT", [128, 8, S], BF16)
    wsl = [sb("wsl%d" % i, [128, 4096], BF16) for i in range(2)]
    tA = sb("tA", [128, S], BF16)
    tB = sb("tB", [128, S], BF16)
    tC = sb("tC", [128, S], BF16)
    tD = sb("tD", [128, S], BF16)
    tE = sb("tE", [128, S], BF16)
    vt = sb("vt", [128, 16, 192], BF16)
    f1 = sb("f1", [128, S], F32)
    f2 = sb("f2", [128, S], F32)
    cmat_f = sb("cmat_s", [128, 6 * 128 + 16 * 70], BF16)
    cmat = cmat_f[:, 0:768].rearrange("p (a b) -> p a b", a=6)
    selm = cmat_f[:, 768:768 + 1120].rearrange("p (a b) -> p a b", a=16)
    ptall = sb("ptall", [128, 6, 512], BF16)
    G = [sb("g%d" % i, [128, 512], F32) for i in range(3)]
    H = [sb("h%d" % i, [128, 512], BF16) for i in range(3)]
    par = sb("par_s", [128, DEPTH * PW], F32)
    cpar = sb("cpar_s", [128, 4], F32)
    sm = sb("sm", [128, 48], F32)

    banks = [es.enter_context(nc.psum_tensor("bk%d" % i, [128, 512], F32)) for i in range(8)]
    free_list = list(range(8))

    R = {}

    def res(name, excl=False):
        if name not in R:
            R[name] = Res(name, excl)
        return R[name]

    r_bk = [res("BK%d" % i, True) for i in range(8)]

    def balloc():
        i = free_list.pop(0)
        return banks[i], r_bk[i], i

    def bfree(i):
        free_list.append(i)

    r_x = [res("x%d" % b) for b in range(16)]
    r_hT = [res("hT%d" % b) for b in range(16)]
    r_og = [[res("og%d_%d" % (h_, c)) for c in range(4)] for h_ in range(8)]
    r_w = [res("w%d" % i) for i in range(2)]
    r_A = [res("A%d" % c) for c in range(4)]
    r_B = [res("B%d" % c) for c in range(4)]
    r_C = [res("C%d" % c) for c in range(4)]
    r_D = [res("D%d" % c) for c in range(4)]
    r_E = [res("E%d" % c) for c in range(4)]
    r_v = [res("v%d" % c) for c in range(4)]
    r_f1 = [res("f1_%d" % c) for c in range(4)]
    r_f2 = [res("f2_%d" % c) for c in range(4)]
    r_cm = res("cmat")
    r_pt = [res("pt%d" % i) for i in range(6)]
    r_G = [res("G%d" % i) for i in range(2)]
    r_G2 = [res("G2a"), res("G2b")]
    r_H = [res("H%d" % i) for i in range(3)]
    r_par = res("par")
    r_sm = res("sm")
    r_one = res("ones64")
    r_junk = res("junk")
    r_junk.strict = True

    ident = cmat[:, 0, :]
    bo64 = cmat[:, 1, :]
    rmat = cmat[:, 2, :]
    mcur = cmat[:, 3, :]
    epsc = par[:, DEPTH * PW - 1:DEPTH * PW]
    onec = par[:, DEPTH * PW - 2:DEPTH * PW - 1]

    sqT = [H[0], H[1]]
    r_sq = [r_H[0], r_H[1]]
    rsT = [G[0], G[1]]
    r_rs = [r_G[0], r_G[1]]
    g2b = G[2][:].bitcast(BF16)
    t1T = [g2b[:, 0:512], g2b[:, 0:512]]
    r_t1 = [r_G2[0], r_G2[0]]
    qnTs = [g2b[:, 512:1024], H[2][:, :]]
    r_qn = [r_G2[1], r_H[2]]
    jobn = [0]

    def chunks_of(lo, hi):
        return list(range(lo // 512, (hi - 1) // 512 + 1))

    sc.dma("pool", "cm", lambda e: e.dma_start(out=cmat_f[:], in_=cmat_d[:, :]),
           (), (r_cm,))
    sc.dma("sp", "par", lambda e: e.dma_start(out=par[:], in_=par_d[:, :]), (), (r_par,))
    sc.dma("sp", "par", lambda e: e.dma_start(out=cpar[:], in_=cpar_d[:, :]), (), (r_par,))
    for b in range(16):
        sc.dma("sp" if b % 2 == 0 else "act", "x%d" % b,
               lambda e, b=b: e.dma_start(out=xres[:, b, :], in_=x_d[b * 128:(b + 1) * 128, :]),
               (), (r_x[b],))
    sc.op("dve", lambda e: e.memset(vt[:, :, 64:128], 1.0), (), tuple(res("v%d" % c) for c in range(4)))
    for t_, r_ in ((tA, r_A), (tB, r_B), (tC, r_C), (tD, r_D), (tE, r_E)):
        sc.op("dve", lambda e, t_=t_: e.memset(t_[:], 0.0), (), tuple(r_))

    wq = {"n": 0}

    def load_w(dram_ap, ncols=4096):
        i = wq["n"] % 2
        wq["n"] += 1
        sc.dma("pool", "w%d" % i,
               lambda e, i=i, a=dram_ap, n=ncols: e.dma_start(out=wsl[i][:, 0:n], in_=a),
               (), (r_w[i],))
        return wsl[i], r_w[i]

    def rmsnorm_prep(l):
        sc.op("dve", lambda e: e.memset(sm[:, 0:16], 0.0), (), (r_sm,))

    def rms_a(l, b):
        ptf = ptall[:].rearrange("p a b -> p (a b)")
        xn = (ptf[:, 0:1024], ptf[:, 1024:2048])[b % 2]
        rxn = ((r_pt[0], r_pt[1]), (r_pt[2], r_pt[3]))[b % 2]
        junk = f2[:].bitcast(BF16)[:, 0:1024]
        sc.op("act", lambda e: e.activation(out=junk, in_=xres[:, b, :], func=AF.Square,
                                            accum_out=sm[:, b:b + 1]), (r_x[b],), (r_f2[0], r_junk, r_sm))
        sc.op("act", lambda e: e.activation(out=sm[:, 16 + b:17 + b], in_=sm[:, b:b + 1], func=AF.Ln,
                                            scale=1.0 / D, bias=epsc), (r_sm, r_par), (r_sm,))
        sc.op("act", lambda e: e.activation(out=sm[:, 16 + b:17 + b], in_=sm[:, 16 + b:17 + b], func=AF.Exp,
                                            scale=-0.5), (r_sm,), (r_sm,))
        sc.op("dve", lambda e: e.tensor_scalar(out=xn, in0=xres[:, b, :], scalar1=sm[:, 16 + b:17 + b],
                                               scalar2=None, op0=ALU.mult), (r_x[b], r_sm), rxn)

    def rms_b(l, b):
        gcols = par[:, l * PW:l * PW + 8]
        ptf = ptall[:].rearrange("p a b -> p (a b)")
        xn = (ptf[:, 0:1024], ptf[:, 1024:2048])[b % 2]
        rxn = ((r_pt[0], r_pt[1]), (r_pt[2], r_pt[3]))[b % 2]
        pj, rpj, bi = balloc()
        pjb = pj[:].bitcast(BF16)
        for dc in range(8):
            sc.op("pe", lambda e, dc=dc: e.transpose(pjb[:, dc * 128:(dc + 1) * 128],
                                                     xn[:, dc * 128:(dc + 1) * 128], ident),
                  rxn + (r_cm,), (rpj,))
        sc.op("dve", lambda e: e.tensor_tensor(
            out=hT[:, :, b * 128:(b + 1) * 128],
            in0=pjb[:, 0:1024].rearrange("p (c t) -> p c t", c=8),
            in1=gcols.unsqueeze(2).to_broadcast([128, 8, 128]), op=ALU.mult),
              (rpj, r_par), (r_hT[b],))
        bfree(bi)

    def rmsnorm_to_hT(l):
        if l != layers[0]:
            return
        rmsnorm_prep(l)
        run_pipeline([[(lambda b=b: rms_a(l, b)), (lambda b=b: rms_b(l, b))] for b in range(16)])

    def proj_mm(wt, rw, u, c, pj, rpj, M=128, wcols=None):
        for dc in range(8):
            if wcols is None:
                lhs = wt[:, dc * 512 + u * 128: dc * 512 + u * 128 + M]
            else:
                lhs = wt[:, dc * wcols: dc * wcols + M]
            sc.op("pe", lambda e, dc=dc, lhs=lhs: e.matmul(pj[0:M, :], lhs, hT[:, dc, c * 512:(c + 1) * 512],
                                                            start=(dc == 0), stop=(dc == 7)),
                  (rw,) + tuple(r_hT[4 * c:4 * c + 4]), (rpj,))

    def gate_job(wt, rw, c, hp):
        def st0():
            pj, rpj, bi = balloc()
            proj_mm(wt, rw, 3, c, pj, rpj)
            sc.op("act", lambda e: e.activation(out=ogT[:, hp, c * 512:(c + 1) * 512], in_=pj[:], func=AF.Silu),
                  (rpj,), (r_og[hp][c],))
            bfree(bi)
        return [st0]

    def v_job(wt, rw, tokslices, g4):
        def st0():
            pj, rpj, bi = balloc()
            for s4 in range(4):
                sl = tokslices[4 * g4 + s4]
                for dc in range(8):
                    sc.op("pe", lambda e, dc=dc, s4=s4, sl=sl: e.matmul(
                        pj[:, s4 * 128:(s4 + 1) * 128], hT[:, dc, sl], wt[:, dc * 512 + 256: dc * 512 + 384],
                        start=(dc == 0), stop=(dc == 7)),
                          (rw,) + tuple(r_hT), (rpj,))
            sc.op("act", lambda e: e.copy(
                out=vt[:, 4 * g4:4 * g4 + 4, :].rearrange("p b (t c) -> p b t c", t=3)[:, :, 0:3:2, :],
                in_=pj[:].rearrange("p (b t c) -> p b t c", b=4, t=2)), (rpj,), (r_v[g4],))
            bfree(bi)
        return [st0]

    def qk_job(wt, rw, u, c, gcol, rg, dests, rot):
        n = jobn[0]
        jobn[0] += 1
        sq, rsq = sqT[n % 2], r_sq[n % 2]
        rs, rrs = rsT[n % 2], r_rs[n % 2]
        t1, rt1 = t1T[n % 2], r_t1[n % 2]
        cs = slice(c * 512, (c + 1) * 512)
        st = {}
        qnT, rqn = qnTs[n % 2], r_qn[n % 2]

        def st0():
            pj, rpj, bi = balloc()
            st["p"] = (pj, rpj, bi)
            proj_mm(wt, rw, u, c, pj, rpj)
            sc.op("act", lambda e: e.activation(out=sq[:], in_=pj[:], func=AF.Square), (rpj,), (rsq,))

        def st1():
            pj, rpj, bi = st["p"]
            pq, rpq, qi = balloc()
            sc.op("pe", lambda e: e.matmul(pq[:], bo64, sq[:], start=True, stop=True), (rsq, r_cm), (rpq,))
            sc.op("act", lambda e: e.activation(out=rs[:], in_=pq[:], func=AF.Ln, bias=epsc), (rpq, r_par), (rrs,))
            bfree(qi)
            sc.op("act", lambda e: e.activation(out=rs[:], in_=rs[:], func=AF.Exp, scale=-0.5), (rrs,), (rrs,))
            for (dt_, lo, hi, rd) in dests:
                sc.op("dve", lambda e, dt_=dt_, lo=lo, hi=hi: e.scalar_tensor_tensor(
                    out=dt_[lo:hi, cs], in0=pj[lo:hi, :], scalar=gcol[lo:hi, :], in1=rs[lo:hi, :],
                    op0=ALU.mult, op1=ALU.mult), (rpj, rg, rrs), (rd[c],))
            bfree(bi)

        def st1_full():
            pj, rpj, bi = st["p"]
            pq, rpq, qi = balloc()
            sc.op("pe", lambda e: e.matmul(pq[:], bo64, sq[:], start=True, stop=True), (rsq, r_cm), (rpq,))
            sc.op("act", lambda e: e.activation(out=rs[:], in_=pq[:], func=AF.Ln, bias=epsc), (rpq, r_par), (rrs,))
            bfree(qi)
            sc.op("act", lambda e: e.activation(out=rs[:], in_=rs[:], func=AF.Exp, scale=-0.5), (rrs,), (rrs,))
            sc.op("dve", lambda e: e.scalar_tensor_tensor(
                out=qnT, in0=pj[:], scalar=gcol, in1=rs[:], op0=ALU.mult, op1=ALU.mult),
                  (rpj, rg, rrs), (rqn,))
            bfree(bi)

        def st2_full():
            pr, rpr, ri = balloc()
            sc.op("pe", lambda e: e.matmul(pr[:], rmat, qnT, start=True, stop=True), (r_cm, rqn), (rpr,))
            sc.op("dve", lambda e: e.tensor_tensor(out=t1T[0], in0=pr[:], in1=tE[:, cs], op=ALU.mult),
                  (rpr, r_E[c]), (r_G2[0],))
            bfree(ri)
            sc.op("dve", lambda e: e.tensor_tensor(out=qnT, in0=qnT, in1=tD[:, cs], op=ALU.mult),
                  (rqn, r_D[c]), (rqn,))
            for (dt_, lo, hi, rd) in dests:
                sc.op("dve", lambda e, dt_=dt_, lo=lo, hi=hi: e.tensor_tensor(
                    out=dt_[lo:hi, cs], in0=qnT[lo:hi, :], in1=t1T[0][lo:hi, :], op=ALU.add),
                      (rqn, r_G2[0]), (rd[c],))

        if rot and len(dests) > 1:
            return [st0, st1_full, st2_full]

        def st2():
            pr, rpr, ri = balloc()
            for i, (dt_, lo, hi, rd) in enumerate(dests):
                sc.op("pe", lambda e, dt_=dt_, i=i: e.matmul(pr[:], rmat, dt_[:, cs], start=(i == 0),
                                                             stop=(i == len(dests) - 1)),
                      (r_cm, rd[c]), (rpr,))
            sc.op("dve", lambda e: e.tensor_tensor(out=t1, in0=pr[:], in1=tE[:, cs], op=ALU.mult),
                  (rpr, r_E[c]), (rt1,))
            bfree(ri)
            for (dt_, lo, hi, rd) in dests:
                sc.op("dve", lambda e, dt_=dt_, lo=lo, hi=hi: e.tensor_tensor(
                    out=dt_[lo:hi, cs], in0=dt_[lo:hi, cs], in1=tD[lo:hi, cs], op=ALU.mult),
                      (rd[c], r_D[c]), (rd[c],))
                sc.op("dve", lambda e, dt_=dt_, lo=lo, hi=hi: e.tensor_tensor(
                    out=dt_[lo:hi, cs], in0=dt_[lo:hi, cs], in1=t1[lo:hi, :], op=ALU.add),
                      (rd[c], rt1), (rd[c],))

        return [st0, st1, st2] if rot else [st0, st1]

    def epilogue(hp, c, sa, sb_, rA, rB, on_dve=False):
        cs = slice(c * 512, (c + 1) * 512)
        if on_dve:
            sc.op("dve", lambda e: e.reciprocal(out=G[0][0:64, :], in_=sa[64:128, :]), rA, (r_G[0],))
            sc.op("dve", lambda e: e.reciprocal(out=G[0][64:128, :], in_=sb_[0:64, :]), rB, (r_G[0],))
        else:
            sc.op("act", lambda e: e.activation(out=G[0][0:64, :], in_=sa[64:128, :], func=AF.Ln), rA, (r_G[0],))
            sc.op("act", lambda e: e.activation(out=G[0][64:128, :], in_=sb_[0:64, :], func=AF.Ln), rB, (r_G[0],))
            sc.op("act", lambda e: e.activation(out=G[0][:], in_=G[0][:], func=AF.Exp, scale=-1.0),
                  (r_G[0],), (r_G[0],))
        sc.op("dve", lambda e: e.tensor_tensor(out=G[2][0:64, :], in0=sa[0:64, :], in1=G[0][0:64, :], op=ALU.mult),
              tuple(rA) + (r_G[0],), tuple(r_G2))
        sc.op("dve", lambda e: e.tensor_tensor(out=G[2][64:128, :], in0=sb_[64:128, :], in1=G[0][64:128, :], op=ALU.mult),
              tuple(rB) + (r_G[0],), tuple(r_G2))
        sc.op("dve", lambda e: e.tensor_tensor(out=ogT[:, hp, cs], in0=G[2][:], in1=ogT[:, hp, cs], op=ALU.mult),
              tuple(r_G2) + (r_og[hp][c],), (r_og[hp][c],))

    def out_proj(l):
        li = layers.index(l)
        nxt = layers[li + 1] if li + 1 < len(layers) else None
        ws = [load_w(wout_d[l, nh, :, :]) for nh in range(2)]
        if nxt is not None:
            rmsnorm_prep(nxt)

        def out_block(tb):
            for nh in range(2):
                wt, rw = ws[nh]
                pj, rpj, bi = balloc()
                for ic in range(8):
                    sc.op("pe", lambda e, ic=ic, pj=pj, wt=wt: e.matmul(
                        pj[:], ogT[:, ic, tb * 128:(tb + 1) * 128], wt[:, ic * 512:(ic + 1) * 512],
                        start=(ic == 0), stop=(ic == 7)),
                          (rw, r_og[ic][tb // 4]), (rpj,))
                sc.op("dve", lambda e, nh=nh, pj=pj: e.tensor_tensor(
                    out=xres[:, tb, nh * 512:(nh + 1) * 512], in0=pj[:],
                    in1=xres[:, tb, nh * 512:(nh + 1) * 512], op=ALU.add),
                      (rpj, r_x[tb]), (r_x[tb],))
                bfree(bi)

        jobs = []
        for tb in range(16):
            j_ = [lambda tb=tb: out_block(tb)]
            if nxt is not None:
                j_ += [(lambda tb=tb: rms_a(nxt, tb)), (lambda tb=tb: rms_b(nxt, tb))]
            jobs.append(j_)
        run_pipeline(jobs)

    def fox_layer(l):
        j = l // 2
        pb = l * PW
        rmsnorm_to_hT(l)
        sc.op("dve", lambda e: e.memset(tC[64:128, :], 0.0), (), tuple(r_C))
        sc.op("dve", lambda e: e.memset(tD[0:64, :], 0.0), (), tuple(r_D))
        sc.op("dve", lambda e: e.memset(tA[64:128, :], 0.0), (), tuple(r_A))
        sc.op("dve", lambda e: e.memset(tB[0:64, :], 0.0), (), tuple(r_B))
        sc.op("dve", lambda e: e.tensor_scalar(out=sm[:, 32:33], in0=par[:, pb + 8:pb + 9], scalar1=0.125,
                                               scalar2=None, op0=ALU.mult), (r_par,), (r_sm,))
        wt, rw = load_w(wff_d[j, :, :], 768)
        for c in range(4):
            pj, rpj, bi = balloc()
            proj_mm(wt, rw, 0, c, pj, rpj, M=96, wcols=96)
            cs = slice(c * 512, (c + 1) * 512)
            sc.op("dve", lambda e, pj=pj, cs=cs: e.tensor_scalar(out=f1[0:96, cs], in0=pj[0:96, :],
                                                                 scalar1=par[0:96, pb + 10:pb + 11], scalar2=None,
                                                                 op0=ALU.add), (rpj, r_par), (r_f1[c],))
            bfree(bi)
            sc.op("act", lambda e, cs=cs: e.activation(out=f1[0:96, cs], in_=f1[0:96, cs], func=AF.Exp, scale=-1.0),
                  (r_f1[c],), (r_f1[c],))
            sc.op("act", lambda e, cs=cs: e.activation(out=f1[0:96, cs], in_=f1[0:96, cs], func=AF.Ln, bias=1.0),
                  (r_f1[c],), (r_f1[c],))
        for c in range(4):
            cs = slice(c * 512, (c + 1) * 512)
            init = 0.0 if c == 0 else f1[0:96, c * 512 - 1:c * 512]
            sc.op("dve", lambda e, cs=cs, init=init: e.tensor_tensor_scan(
                out=f1[0:96, cs], data0=onec[0:96, :].to_broadcast([96, 512]),
                data1=f1[0:96, cs], initial=init, op0=ALU.mult, op1=ALU.add),
                  (r_f1[c], r_par) + ((r_f1[c - 1],) if c else ()), (r_f1[c],))
        sc.op("dve", lambda e: e.memset(tE[96:97, :], 1.0), (), tuple(r_E))
        for c in range(4):
            cs = slice(c * 512, (c + 1) * 512)
            sc.op("dve", lambda e, cs=cs: e.tensor_copy(out=H[0][0:96, :], in_=f1[0:96, cs]), (r_f1[c],), (r_H[0],))
            sc.op("dve", lambda e, cs=cs: e.tensor_tensor(out=G[0][0:96, :], in0=f1[0:96, cs], in1=H[0][0:96, :],
                                                          op=ALU.subtract), (r_f1[c], r_H[0]), (r_G[0],))
            sc.op("dve", lambda e: e.tensor_copy(out=H[1][0:96, :], in_=G[0][0:96, :]), (r_G[0],), (r_H[1],))
            sc.op("dve", lambda e: e.tensor_tensor(out=G[1][0:96, :], in0=G[0][0:96, :], in1=H[1][0:96, :],
                                                   op=ALU.subtract), (r_G[0], r_H[1]), (r_G[1],))
            sc.op("dve", lambda e, cs=cs: e.tensor_copy(out=tE[0:16, cs], in_=H[0][0:16, :]), (r_H[0],), (r_E[c],))
            sc.op("dve", lambda e, cs=cs: e.tensor_copy(out=tE[32:48, cs], in_=H[1][32:48, :]), (r_H[1],), (r_E[c],))
            sc.op("dve", lambda e, cs=cs: e.tensor_copy(out=tE[64:80, cs], in_=G[1][64:80, :]), (r_G[1],), (r_E[c],))


        def aug_job(hp, side, c):
            sel = selm[:, side * 8 + hp, :]
            dA, dB = (tA, tB) if side == 0 else (tC, tD)
            rdA, rdB = (r_A, r_B) if side == 0 else (r_C, r_D)
            cs = slice(c * 512, (c + 1) * 512)

            def st0():
                pj, rpj, bi = balloc()
                sc.op("pe", lambda e: e.matmul(pj[0:70, :], sel, tE[:, cs], start=True, stop=True), (r_cm, r_E[c]), (rpj,))
                sc.op("dve", lambda e: e.tensor_copy(out=dA[64:70, cs], in_=pj[64:70, :]), (rpj,), (rdA[c],))
                sc.op("dve", lambda e: e.tensor_copy(out=dB[0:6, cs], in_=pj[0:6, :]), (rpj,), (rdB[c],))
                bfree(bi)
            return [st0]

        cidx = 0
        for hp in range(8):
            wt, rw = load_w(wfox_d[j, hp, :, :])
            jobs = [gate_job(wt, rw, c, hp) for c in range(4)]
            for side in (0, 1):
                for c in range(4):
                    jobs.append(aug_job(hp, side, c))
            for c in range(4):
                jobs.append(qk_job(wt, rw, 0, c, sm[:, 32:33], r_sm,
                                   [(tA, 0, 64, r_A), (tB, 64, 128, r_B)], False))
            for c in range(4):
                jobs.append(qk_job(wt, rw, 1, c, par[:, pb + 9:pb + 10], r_par,
                                   [(tC, 0, 64, r_C), (tD, 64, 128, r_D)], False))
            tks = [slice(b * 128, (b + 1) * 128) for b in range(16)]
            for g4 in range(4):
                jobs.append(v_job(wt, rw, tks, g4))
            run_pipeline(jobs)
            ul = [(balloc(), balloc()) for _ in range(2)]
            units = [(c, kb) for c in range(4) for kb in range(4 * c + 4)]
            pendq = []
            for ui in range(len(units) + 2):
                if ui < len(units):
                    c, kb = units[ui]
                    w = 512 if kb < 4 * c else (4 * c + 4 - kb) * 128
                    q0 = 512 * (c + 1) - w
                    sa, rsa, sai = balloc()
                    sbk, rsb, sbi = balloc()
                    pa, pb_ = ptall[:, (2 * ui) % 6, :], ptall[:, (2 * ui + 1) % 6, :]
                    rpa, rpb = r_pt[(2 * ui) % 6], r_pt[(2 * ui + 1) % 6]
                    ks = slice(kb * 128, (kb + 1) * 128)
                    qs = slice(q0, q0 + w)
                    qch = chunks_of(q0, q0 + w)
                    diag = kb >= 4 * c
                    for (st_, mv_, rst, rmv, sbank, rsbank) in ((tC, tA, r_C, r_A, sa, rsa), (tD, tB, r_D, r_B, sbk, rsb)):
                        sc.op("pe", lambda e, st_=st_, mv_=mv_, sbank=sbank, ks=ks, qs=qs, w=w, diag=diag: e.matmul(
                            sbank[:, 0:w], st_[:, ks], mv_[:, qs], start=True, stop=not diag),
                              (rst[kb // 4],) + tuple(rmv[i] for i in qch), (rsbank,))
                        if diag:
                            sc.op("pe", lambda e, sbank=sbank: e.matmul(sbank[:, 0:128], ident, mcur,
                                                                        start=False, stop=True),
                                  (r_cm,), (rsbank,))
                    sc.op("act", lambda e, pa=pa, sa=sa, w=w: e.activation(out=pa[:, 0:w], in_=sa[:, 0:w], func=AF.Exp),
                          (rsa,), (rpa,))
                    sc.op("act", lambda e, pb_=pb_, sbk=sbk, w=w: e.activation(out=pb_[:, 0:w], in_=sbk[:, 0:w], func=AF.Exp),
                          (rsb,), (rpb,))
                    bfree(sai)
                    bfree(sbi)
                    pendq.append((c, kb, w, pa, pb_, rpa, rpb))
                if ui >= 2 and pendq:
                    c_, kb_, w_, pa_, pbb_, rpa_, rpb_ = pendq.pop(0)
                    (UB, r_UB, _u), (LB, r_LB, _l) = ul[c_ % 2]
                    nkb = 4 * c_ + 4
                    first = kb_ == 0
                    last = kb_ == nkb - 1
                    osl = slice(512 - w_, 512)
                    rl = r_v[kb_ // 4]
                    sc.op("pe", lambda e, UB=UB, kb_=kb_, pa_=pa_, osl=osl, w_=w_, first=first, last=last: e.matmul(
                        UB[:, osl], vt[:, kb_, 0:128], pa_[:, 0:w_], start=first, stop=last, skip_group_check=True),
                          (rl, rpa_), (r_UB,))
                    sc.op("pe", lambda e, LB=LB, kb_=kb_, pbb_=pbb_, osl=osl, w_=w_, first=first, last=last: e.matmul(
                        LB[:, osl], vt[:, kb_, 64:192], pbb_[:, 0:w_], start=first, stop=last, skip_group_check=True),
                          (rl, rpb_), (r_LB,))
                    if last:
                        epilogue(hp, c_, UB, LB, (r_UB,), (r_LB,), on_dve=True)
            for (a_, b_) in ul:
                bfree(a_[2])
                bfree(b_[2])
        out_proj(l)

    def rotary_tables():
        for c in range(4):
            cs = slice(c * 512, (c + 1) * 512)
            gi = G[0][:].bitcast(I32)
            sc.dma("sp", "pos", lambda e, cs=cs, gi=gi: e.dma_start(out=gi, in_=pos_d[:, cs]), (), (r_G[0],))
            sc.op("dve", lambda e, gi=gi: e.tensor_copy(out=G[1][:], in_=gi), (r_G[0],), (r_G[1],))
            for which in (0, 1):
                dst, rdst = (tD, r_D) if which == 0 else (tE, r_E)
                sc.op("dve", lambda e, which=which: e.tensor_scalar(
                    out=G[2][:], in0=G[1][:], scalar1=cpar[:, 0:1], scalar2=(0.25 if which == 0 else 0.0),
                    op0=ALU.mult, op1=ALU.add), (r_G[1], r_par), tuple(r_G2))
                sc.op("dve", lambda e, gi=gi: e.tensor_copy(out=gi, in_=G[2][:]), tuple(r_G2), (r_G[0],))
                sc.op("dve", lambda e, gi=gi: e.tensor_copy(out=f2[:, 0:512], in_=gi), (r_G[0],), (r_f2[0],))
                sc.op("dve", lambda e: e.tensor_tensor(out=G[2][:], in0=G[2][:], in1=f2[:, 0:512], op=ALU.subtract),
                      tuple(r_G2) + (r_f2[0],), tuple(r_G2))
                if which == 0:
                    sc.op("act", lambda e, dst=dst, cs=cs: e.activation(out=dst[:, cs], in_=G[2][:], func=AF.Sin,
                                                                         scale=float(2 * np.pi)),
                          tuple(r_G2), (rdst[c],))
                else:
                    sc.op("act", lambda e, dst=dst, cs=cs: e.activation(out=dst[:, cs], in_=G[2][:], func=AF.Sin,
                                                                         scale=cpar[:, 1:2]),
                          tuple(r_G2) + (r_par,), (rdst[c],))

    def dsw_layer(l):
        j = l // 2
        pb = l * PW
        rmsnorm_to_hT(l)
        rotary_tables()
        sc.op("dve", lambda e: e.memset(tA[64:128, :], 0.0), (), tuple(r_A))
        sc.op("dve", lambda e: e.memset(tB[0:64, :], 0.0), (), tuple(r_B))
        sc.op("dve", lambda e: e.tensor_scalar(out=sm[:, 33:36], in0=par[:, pb + 8:pb + 11], scalar1=0.125,
                                               scalar2=None, op0=ALU.mult), (r_par,), (r_sm,))
        for i_, m_ in enumerate((3, 3, 4, 4)):
            sc.op("dve", lambda e, i_=i_, m_=m_: e.tensor_copy(out=ptall[:, 3, i_ * 128:(i_ + 1) * 128], in_=cmat[:, m_, :]),
                  (r_cm,), (r_pt[3],))
        qidx = 0
        for hp in range(8):
            for g, (window, r) in enumerate(DSWA):
                nb = 16 // r
                wt, rw = load_w(wdsw_d[j, hp * 3 + g, :, :])

                def tsl(rho, jb, r=r):
                    st0 = rho + r * 128 * jb
                    return slice(st0, st0 + 127 * r + 1, r) if r > 1 else slice(st0, st0 + 128)

                blks = [(rho, jb) for rho in range(r) for jb in range(nb)]
                jobs = []
                if g == 0:
                    jobs += [gate_job(wt, rw, c, hp) for c in range(4)]
                for c in range(4):
                    jobs.append(qk_job(wt, rw, 0, c, sm[:, 33 + g:34 + g], r_sm,
                                       [(tA, 0, 64, r_A), (tB, 64, 128, r_B)], True))
                for c in range(4):
                    jobs.append(qk_job(wt, rw, 1, c, par[:, pb + 11 + g:pb + 12 + g], r_par,
                                       [(tC, 0, 128, r_C)], True))
                tks = [tsl(rho, jb) for (rho, jb) in blks]
                for g4 in range(4):
                    jobs.append(v_job(wt, rw, tks, g4))
                run_pipeline(jobs)

                ul = [(balloc(), balloc()) for _ in range(2)]
                ajobs = []
                for bi_ in range(16):
                    def mk(bi_=bi_, qd0=qidx):
                        qd, s4 = bi_ // 4, bi_ % 4
                        rho, jb = blks[bi_]
                        keys = [bi_] + ([bi_ - 1] if jb > 0 else [])
                        ncol = 256 * len(keys)
                        pt, rpt = ptall[:, bi_ % 3, :], r_pt[bi_ % 3]
                        qsl = tsl(rho, jb)
                        (UB, r_UB, _u), (LB, r_LB, _l) = ul[(qd0 + qd) % 2]

                        def st0():
                            sbank, rsbank, si = balloc()
                            first_mm = True
                            for ki, kblk in enumerate(keys):
                                ksl = tsl(*blks[kblk])
                                for h, (mv_, rmv) in enumerate(((tA, r_A), (tB, r_B))):
                                    col = (2 * ki + h) * 128
                                    sc.op("pe", lambda e, col=col, ksl=ksl, mv_=mv_, fm=first_mm: e.matmul(
                                        sbank[:, col:col + 128], tC[:, ksl], mv_[:, qsl], start=fm, stop=False,
                                        skip_group_check=True),
                                          tuple(r_C) + tuple(rmv), (rsbank,))
                                    first_mm = False
                            sc.op("pe", lambda e: e.matmul(sbank[:, 0:ncol], ident, ptall[:, 3, 0:ncol], start=False,
                                                           stop=True, skip_group_check=True),
                                  (r_cm, r_pt[3]), (rsbank,))
                            sc.op("act", lambda e: e.activation(out=pt[:, 0:ncol], in_=sbank[:, 0:ncol], func=AF.Exp),
                                  (rsbank,), (rpt,))
                            bfree(si)

                        def st1():
                            for ki, kblk in enumerate(keys):
                                rl = r_v[kblk // 4]
                                fs = (s4 == 0 and ki == 0)
                                sc.op("pe", lambda e, ki=ki, kblk=kblk, fs=fs: e.matmul(
                                    UB[:, s4 * 128:(s4 + 1) * 128], vt[:, kblk, 0:128], pt[:, (2 * ki) * 128:(2 * ki + 1) * 128],
                                    start=fs, stop=False, skip_group_check=True), (rl, rpt), (r_UB,))
                                sc.op("pe", lambda e, ki=ki, kblk=kblk, fs=fs: e.matmul(
                                    LB[:, s4 * 128:(s4 + 1) * 128], vt[:, kblk, 64:192], pt[:, (2 * ki + 1) * 128:(2 * ki + 2) * 128],
                                    start=fs, stop=False, skip_group_check=True), (rl, rpt), (r_LB,))
                            if s4 == 3:
                                if r == 1:
                                    av = lambda t: t[:, qd * 512:(qd + 1) * 512]
                                    rch = (qd,)
                                    pv = lambda b: b[:]
                                elif r == 4:
                                    av = lambda t: t[:, qd:S:4]
                                    rch = (0, 1, 2, 3)
                                    pv = lambda b: b[:]
                                else:
                                    av = lambda t: t[:].rearrange("p (i r) -> p r i", r=16)[:, 4 * qd:4 * qd + 4, :]
                                    rch = (0, 1, 2, 3)
                                    pv = lambda b: b[:].rearrange("p (s i) -> p s i", s=4)
                                for (bank, rbank, acc, racc) in ((UB, r_UB, f1, r_f1), (LB, r_LB, f2, r_f2)):
                                    if g == 0:
                                        sc.op("act", lambda e, bank=bank, acc=acc: e.copy(out=av(acc), in_=pv(bank)),
                                              (rbank,), tuple(racc[i] for i in rch))
                                    else:
                                        sc.op("dve", lambda e, bank=bank, acc=acc: e.tensor_tensor(
                                            out=av(acc), in0=pv(bank), in1=av(acc), op=ALU.add),
                                              (rbank,) + tuple(racc[i] for i in rch), tuple(racc[i] for i in rch))
                        return [st0, (lambda: None), st1]
                    ajobs.append(mk())
                run_pipeline(ajobs)
                for (a_, b_) in ul:
                    bfree(a_[2])
                    bfree(b_[2])
                qidx += 4
            for c in range(4):
                cs = slice(c * 512, (c + 1) * 512)
                epilogue(hp, c, f1[:, cs], f2[:, cs], (r_f1[c],), (r_f2[c],))
        out_proj(l)

    for l in layers:
        if l % 2 == 0:
            fox_layer(l)
        else:
            dsw_layer(l)

    for b in range(16):
        sc.dma("sp", "o%d" % b,
               lambda e, b=b: e.dma_start(out=y_d[b * 128:(b + 1) * 128, :], in_=xres[:, b, :]),
               (r_x[b],), ())
    outres = []
    for b in range(16):
        rr = Res("oo%d" % b)
        rr.w = ("d", "o%d" % b, 1)
        outres.append(rr)
    sc.wait_all("sp", outres)
    sc.emit(nc, es)
    es.close()
    return nc


def _consts():
    cm = np.zeros((6, 128, 128), np.float32)
    cm[0] = np.eye(128)
    p = np.arange(128)
    cm[1] = (p[:, None] // 64 == p[None, :] // 64) / 64.0
    for base in (0, 64):
        for i in range(8):
            cm[2][base + i + 8, base + i] = 1.0
            cm[2][base + i, base + i + 8] = 1.0
    k = p[:, None]
    q = p[None, :]
    cm[3] = np.where(q >= k, 0.0, NEG)
    cm[4] = np.where(q <= k, 0.0, NEG)
    sel = np.zeros((16, 128, 128), np.float32)
    for hp in range(8):
        a, b = 2 * hp, 2 * hp + 1
        sq = sel[hp]
        sk = sel[8 + hp]
        for jj in range(3):
            sq[32 * jj + a, 64 + jj] = -1.0
            sq[96, 64 + 3 + jj] = 1.0
            sq[32 * jj + b, jj] = -1.0
            sq[96, 3 + jj] = 1.0
            sk[96, 64 + jj] = 1.0
            sk[32 * jj + a, 64 + 3 + jj] = 1.0
            sk[96, jj] = 1.0
            sk[32 * jj + b, 3 + jj] = 1.0
    cmat = np.ascontiguousarray(np.concatenate(
        [cm.transpose(1, 0, 2).reshape(128, 6 * 128), sel[:, :, 0:70].transpose(1, 0, 2).reshape(128, 16 * 70)], axis=1))
    inv_freq = (np.float32(500000.0) ** (-np.arange(0, 16, 2, dtype=np.float32) / np.float32(16))).astype(np.float32)
    cpar = np.zeros((128, 4), np.float32)
    for base in (0, 64):
        for i in range(8):
            cpar[base + i, 0] = inv_freq[i] / (2 * np.pi)
            cpar[base + i + 8, 0] = inv_freq[i] / (2 * np.pi)
            cpar[base + i, 1] = -2 * np.pi
            cpar[base + i + 8, 1] = 2 * np.pi
    return cmat, cpar


def _prep_shared(norm_g, fox_w_in, fox_b_f, fox_q_norm, fox_k_norm, fox_w_out,
                 dsw_w_in, dsw_q_norm, dsw_k_norm, dsw_w_out):
    f = np.float32
    par = np.zeros((128, DEPTH * PW), f)
    p = np.arange(128)
    for l in range(DEPTH):
        par[:, l * PW:l * PW + 8] = np.asarray(norm_g[l], f).reshape(8, 128).T
        j = l // 2
        if l % 2 == 0:
            par[:, l * PW + 8] = np.asarray(fox_q_norm[j], f)[p % 64]
            par[:, l * PW + 9] = np.asarray(fox_k_norm[j], f)[p % 64]
            for base in (0, 32, 64):
                par[base:base + 16, l * PW + 10] = np.asarray(fox_b_f[j], f)
        else:
            for g in range(3):
                par[:, l * PW + 8 + g] = np.asarray(dsw_q_norm[j, g], f)[p % 64]
                par[:, l * PW + 11 + g] = np.asarray(dsw_k_norm[j, g], f)[p % 64]
    par[:, DEPTH * PW - 1] = EPS
    par[:, DEPTH * PW - 2] = 1.0
    w = np.asarray(fox_w_in, f)
    w4 = w[:, :, :4096].reshape(2, 8, 128, 4, 8, 128)
    wfox = np.ascontiguousarray(w4.transpose(0, 4, 2, 1, 3, 5)).reshape(2, 8, 128, 4096)
    wf = w[:, :, 4096:4112].reshape(2, 8, 128, 16)
    wff = np.zeros((2, 128, 8, 96), f)
    for base in (0, 32, 64):
        wff[:, :, :, base:base + 16] = wf.transpose(0, 2, 1, 3)
    wff = wff.reshape(2, 128, 768)
    wd = np.asarray(dsw_w_in, f).reshape(2, 8, 128, 10240)
    wdsw = np.empty((2, 8, 3, 128, 8, 4, 128), f)
    for hp in range(8):
        for g in range(3):
            for u in range(3):
                c0 = u * 3072 + g * 1024 + hp * 128
                wdsw[:, hp, g, :, :, u, :] = wd[:, :, :, c0:c0 + 128].transpose(0, 2, 1, 3)
            c0 = 9216 + hp * 128
            wdsw[:, hp, g, :, :, 3, :] = wd[:, :, :, c0:c0 + 128].transpose(0, 2, 1, 3)
    wdsw = wdsw.reshape(2, 24, 128, 4096)
    wout = np.empty((4, 2, 128, 8, 512), f)
    for l in range(DEPTH):
        wo = np.asarray(fox_w_out[l // 2] if l % 2 == 0 else dsw_w_out[l // 2], f).reshape(8, 128, 2, 512)
        wout[l] = wo.transpose(2, 1, 0, 3)
    wout = wout.reshape(4, 2, 128, 4096)
    cmat, cpar = _consts()
    return {"par": par, "cpar": cpar, "cmat": cmat, "wfox": wfox, "wff": wff, "wdsw": wdsw, "wout": wout}


_PROG_CACHE = {}


def _get_prog(layers):
    key = tuple(layers)
    if key not in _PROG_CACHE:
        _PROG_CACHE[key] = build_program(list(layers))
    return _PROG_CACHE[key]


def kernel(x, positions, norm_g, fox_w_in, fox_b_f, fox_q_norm, fox_k_norm, fox_w_out,
           dsw_w_in, dsw_q_norm, dsw_k_norm, dsw_w_out, _groups=None):
    shared = _prep_shared(norm_g, fox_w_in, fox_b_f, fox_q_norm, fox_k_norm, fox_w_out,
                          dsw_w_in, dsw_q_norm, dsw_k_norm, dsw_w_out)
    x = np.asarray(x, np.float32)
    positions = np.asarray(positions, np.int32)
    B = x.shape[0]
    cur = [np.ascontiguousarray(x[b]) for b in range(B)]
    posb = [np.ascontiguousarray(np.broadcast_to(positions[b][None, :], (128, S))) for b in range(B)]
    for layers in (_groups or LAUNCH_GROUPS):
        nc = _get_prog(layers)
        in_maps = []
        for b in range(B):
            m = dict(shared)
            m["x"] = cur[b]
            m["pos"] = posb[b]
            in_maps.append(m)
        res = run_bass_kernel_spmd(nc, in_maps, core_ids=list(range(B)))
        cur = [np.asarray(res.results[b]["y"], np.float32) for b in range(B)]
    return np.stack(cur, axis=0)
```

```python
import numpy as np
from contextlib import ExitStack
import concourse.bass as bass
import concourse.mybir as mybir
from concourse.bass_utils import run_bass_kernel_spmd

F32 = mybir.dt.float32
BF16 = mybir.dt.bfloat16
I32 = mybir.dt.int32
AF = mybir.ActivationFunctionType
ALU = mybir.AluOpType

D = 1024
S = 2048
NH = 16
HD = 64
DEPTH = 4
EPS = 1e-6
NEG = -30000.0
ENGS = ("pe", "act", "dve", "pool", "sp")
LAUNCH_GROUPS = [[0, 1, 2, 3]]
DSWA = ((128, 1), (512, 4), (2048, 16))
PW = 16


class Res:
    __slots__ = ("name", "w", "r", "excl", "strict")

    def __init__(self, name, excl=False, strict=False):
        self.name = name
        self.w = None
        self.r = {}
        self.excl = excl
        self.strict = strict


class Sched:
    def __init__(self):
        self.ops = {e: [] for e in ENGS}
        self.known = {e: {} for e in ENGS}
        self.dcount = {}
        self.dsnap = {}

    def _collect(self, eng, reads, writes):
        deps = {}

        def add(tok, raw):
            if tok is None:
                return
            kind, key, idx = tok
            if kind == "e" and key == eng and (eng == "pe" or not raw):
                return
            k = (kind, key)
            if deps.get(k, -1) < idx:
                deps[k] = idx

        for r in reads:
            add(r.w, True)
            if r.excl:
                for k, idx in r.r.items():
                    if not (k[0] == "e" and k[1] == eng):
                        add((k[0], k[1], idx), False)
        for w in writes:
            add(w.w, w.strict)
            for k, idx in w.r.items():
                add((k[0], k[1], idx), w.strict)
        kn = self.known[eng]
        waits = []
        for k, idx in deps.items():
            if kn.get(k, -1) >= idx:
                continue
            waits.append((k, idx))
        for k, idx in waits:
            if kn.get(k, -1) < idx:
                kn[k] = idx
            snap = self.ops[k[1]][idx]["snap"] if k[0] == "e" else self.dsnap.get((k[1], idx), {})
            for kk, vv in snap.items():
                if kn.get(kk, -1) < vv:
                    kn[kk] = vv
            if k[0] == "e":
                self.ops[k[1]][idx]["sig"] = True
        return waits

    def op(self, eng, fn, reads=(), writes=()):
        waits = self._collect(eng, reads, writes)
        idx = len(self.ops[eng])
        self.ops[eng].append({"kind": "c", "fn": fn, "waits": waits, "sig": False,
                              "snap": dict(self.known[eng])})
        tok = ("e", eng)
        for r in reads:
            r.r[tok] = idx
        for w in writes:
            w.w = ("e", eng, idx)
            w.r = {}

    def dma(self, eng, key, fn, reads=(), writes=()):
        waits = self._collect(eng, reads, writes)
        cnt = self.dcount.get(key, 0) + 1
        self.dcount[key] = cnt
        self.ops[eng].append({"kind": "d", "fn": fn, "waits": waits, "sig": False, "dkey": key,
                              "snap": dict(self.known[eng])})
        self.dsnap[(key, cnt)] = dict(self.known[eng])
        tok = ("d", key)
        for r in reads:
            r.r[tok] = cnt
        for w in writes:
            w.w = ("d", key, cnt)
            w.r = {}

    def wait_all(self, eng, reslist):
        waits = self._collect(eng, reslist, ())
        self.ops[eng].append({"kind": "w", "fn": None, "waits": waits, "sig": False,
                              "snap": dict(self.known[eng])})

    def emit(self, nc, es):
        sems = {e: es.enter_context(nc.semaphore("s_" + e)) for e in ENGS}
        dsems = {k: es.enter_context(nc.semaphore("d_" + k)) for k in self.dcount}
        for e in ENGS:
            c = 0
            for o in self.ops[e]:
                if o["kind"] == "c" and o["sig"]:
                    c += 1
                    o["val"] = c
        ops = self.ops

        def run(e, eng):
            for o in ops[e]:
                wl = [(sems[k[1]], ops[k[1]][idx]["val"]) if k[0] == "e" else (dsems[k[1]], 16 * idx)
                      for k, idx in o["waits"]]
                attach = None
                if o["kind"] != "w" and wl:
                    attach = wl.pop()
                for sm_, v_ in wl:
                    eng.wait_ge(sm_, v_)
                if o["kind"] == "w":
                    continue
                ins = o["fn"](eng)
                if attach is not None:
                    ins._wait_ge(attach[0], attach[1])
                if o["kind"] == "d":
                    ins.then_inc(dsems[o["dkey"]], 16)
                elif o["sig"]:
                    ins.then_inc(sems[e], 1)

        with nc.Block() as block:
            @block.tensor
            def _(t):
                run("pe", t)

            @block.scalar
            def _(a):
                run("act", a)

            @block.vector
            def _(v):
                run("dve", v)

            @block.gpsimd
            def _(g):
                run("pool", g)

            @block.sync
            def _(s):
                run("sp", s)


def run_pipeline(jobs):
    n = len(jobs)
    if n == 0:
        return
    mx = max(len(j) for j in jobs)
    for t in range(n + mx - 1):
        for k in range(mx):
            i = t - k
            if 0 <= i < n and k < len(jobs[i]):
                jobs[i][k]()


def build_program(layers):
    nc = bass.Bass("TRN2", target_bir_lowering=False)
    es = ExitStack()
    sc = Sched()

    x_d = nc.dram_tensor("x", [S, D], F32, kind="ExternalInput").ap()
    y_d = nc.dram_tensor("y", [S, D], F32, kind="ExternalOutput").ap()
    pos_d = nc.dram_tensor("pos", [128, S], I32, kind="ExternalInput").ap()
    par_d = nc.dram_tensor("par", [128, DEPTH * PW], F32, kind="ExternalInput").ap()
    cpar_d = nc.dram_tensor("cpar", [128, 4], F32, kind="ExternalInput").ap()
    cmat_d = nc.dram_tensor("cmat", [128, 6 * 128 + 16 * 70], F32, kind="ExternalInput").ap()
    wfox_d = nc.dram_tensor("wfox", [2, 8, 128, 4096], F32, kind="ExternalInput").ap()
    wff_d = nc.dram_tensor("wff", [2, 128, 8 * 96], F32, kind="ExternalInput").ap()
    wdsw_d = nc.dram_tensor("wdsw", [2, 24, 128, 4096], F32, kind="ExternalInput").ap()
    wout_d = nc.dram_tensor("wout", [4, 2, 128, 4096], F32, kind="ExternalInput").ap()

    def sb(name, shape, dt):
        return es.enter_context(nc.sbuf_tensor(name, shape, dt))

    xres = sb("xres", [128, 16, D], F32)
    hT = sb("hT", [128, 8, S], BF16)
    ogT = sb("ogT", [128, 8, S], BF16)
    wsl = [sb("wsl%d" % i, [128, 4096], BF16) for i in range(2)]
    tA = sb("tA", [128, S], BF16)
    tB = sb("tB", [128, S], BF16)
    tC = sb("tC", [128, S], BF16)
    tD = sb("tD", [128, S], BF16)
    tE = sb("tE", [128, S], BF16)
    vt = sb("vt", [128, 16, 192], BF16)
    f1 = sb("f1", [128, S], F32)
    f2 = sb("f2", [128, S], F32)
    cmat_f = sb("cmat_s", [128, 6 * 128 + 16 * 70], BF16)
    cmat = cmat_f[:, 0:768].rearrange("p (a b) -> p a b", a=6)
    selm = cmat_f[:, 768:768 + 1120].rearrange("p (a b) -> p a b", a=16)
    ptall = sb("ptall", [128, 6, 512], BF16)
    G = [sb("g%d" % i, [128, 512], F32) for i in range(3)]
    H = [sb("h%d" % i, [128, 512], BF16) for i in range(3)]
    par = sb("par_s", [128, DEPTH * PW], F32)
    cpar = sb("cpar_s", [128, 4], F32)
    sm = sb("sm", [128, 48], F32)

    banks = [es.enter_context(nc.psum_tensor("bk%d" % i, [128, 512], F32)) for i in range(8)]
    free_list = list(range(8))

    R = {}

    def res(name, excl=False):
        if name not in R:
            R[name] = Res(name, excl)
        return R[name]

    r_bk = [res("BK%d" % i, True) for i in range(8)]

    def balloc():
        i = free_list.pop(0)
        return banks[i], r_bk[i], i

    def bfree(i):
        free_list.append(i)

    r_x = [res("x%d" % b) for b in range(16)]
    r_hT = [res("hT%d" % b) for b in range(16)]
    r_og = [[res("og%d_%d" % (h_, c)) for c in range(4)] for h_ in range(8)]
    r_w = [res("w%d" % i) for i in range(2)]
    r_A = [res("A%d" % c) for c in range(4)]
    r_B = [res("B%d" % c) for c in range(4)]
    r_C = [res("C%d" % c) for c in range(4)]
    r_D = [res("D%d" % c) for c in range(4)]
    r_E = [res("E%d" % c) for c in range(4)]
    r_v = [res("v%d" % c) for c in range(4)]
    r_f1 = [res("f1_%d" % c) for c in range(4)]
    r_f2 = [res("f2_%d" % c) for c in range(4)]
    r_cm = res("cmat")
    r_pt = [res("pt%d" % i) for i in range(6)]
    r_G = [res("G%d" % i) for i in range(2)]
    r_G2 = [res("G2a"), res("G2b")]
    r_H = [res("H%d" % i) for i in range(3)]
    r_par = res("par")
    r_sm = res("sm")
    r_one = res("ones64")
    r_junk = res("junk")
    r_junk.strict = True

    ident = cmat[:, 0, :]
    bo64 = cmat[:, 1, :]
    rmat = cmat[:, 2, :]
    mcur = cmat[:, 3, :]
    epsc = par[:, DEPTH * PW - 1:DEPTH * PW]
    onec = par[:, DEPTH * PW - 2:DEPTH * PW - 1]

    sqT = [H[0], H[1]]
    r_sq = [r_H[0], r_H[1]]
    rsT = [G[0], G[1]]
    r_rs = [r_G[0], r_G[1]]
    g2b = G[2][:].bitcast(BF16)
    t1T = [g2b[:, 0:512], g2b[:, 0:512]]
    r_t1 = [r_G2[0], r_G2[0]]
    qnTs = [g2b[:, 512:1024], H[2][:, :]]
    r_qn = [r_G2[1], r_H[2]]
    jobn = [0]

    def chunks_of(lo, hi):
        return list(range(lo // 512, (hi - 1) // 512 + 1))

    sc.dma("pool", "cm", lambda e: e.dma_start(out=cmat_f[:], in_=cmat_d[:, :]),
           (), (r_cm,))
    sc.dma("sp", "par", lambda e: e.dma_start(out=par[:], in_=par_d[:, :]), (), (r_par,))
    sc.dma("sp", "par", lambda e: e.dma_start(out=cpar[:], in_=cpar_d[:, :]), (), (r_par,))
    for b in range(16):
        sc.dma("sp", "x%d" % b,
               lambda e, b=b: e.dma_start(out=xres[:, b, :], in_=x_d[b * 128:(b + 1) * 128, :]),
               (), (r_x[b],))
    sc.op("dve", lambda e: e.memset(vt[:, :, 64:128], 1.0), (), tuple(res("v%d" % c) for c in range(4)))
    for t_, r_ in ((tA, r_A), (tB, r_B), (tC, r_C), (tD, r_D), (tE, r_E)):
        sc.op("dve", lambda e, t_=t_: e.memset(t_[:], 0.0), (), tuple(r_))

    wq = {"n": 0}

    def load_w(dram_ap, ncols=4096):
        i = wq["n"] % 2
        wq["n"] += 1
        sc.dma("pool", "w%d" % i,
               lambda e, i=i, a=dram_ap, n=ncols: e.dma_start(out=wsl[i][:, 0:n], in_=a),
               (), (r_w[i],))
        return wsl[i], r_w[i]

    def rmsnorm_prep(l):
        sc.op("dve", lambda e: e.memset(sm[:, 0:16], 0.0), (), (r_sm,))

    def rms_a(l, b):
        ptf = ptall[:].rearrange("p a b -> p (a b)")
        xn = (ptf[:, 0:1024], ptf[:, 1024:2048])[b % 2]
        rxn = ((r_pt[0], r_pt[1]), (r_pt[2], r_pt[3]))[b % 2]
        junk = f2[:].bitcast(BF16)[:, 0:1024]
        sc.op("act", lambda e: e.activation(out=junk, in_=xres[:, b, :], func=AF.Square,
                                            accum_out=sm[:, b:b + 1]), (r_x[b],), (r_f2[0], r_junk, r_sm))
        sc.op("act", lambda e: e.activation(out=sm[:, 16 + b:17 + b], in_=sm[:, b:b + 1], func=AF.Ln,
                                            scale=1.0 / D, bias=epsc), (r_sm, r_par), (r_sm,))
        sc.op("act", lambda e: e.activation(out=sm[:, 16 + b:17 + b], in_=sm[:, 16 + b:17 + b], func=AF.Exp,
                                            scale=-0.5), (r_sm,), (r_sm,))
        sc.op("dve", lambda e: e.tensor_scalar(out=xn, in0=xres[:, b, :], scalar1=sm[:, 16 + b:17 + b],
                                               scalar2=None, op0=ALU.mult), (r_x[b], r_sm), rxn)

    def rms_b(l, b):
        gcols = par[:, l * PW:l * PW + 8]
        ptf = ptall[:].rearrange("p a b -> p (a b)")
        xn = (ptf[:, 0:1024], ptf[:, 1024:2048])[b % 2]
        rxn = ((r_pt[0], r_pt[1]), (r_pt[2], r_pt[3]))[b % 2]
        pj, rpj, bi = balloc()
        pjb = pj[:].bitcast(BF16)
        for dc in range(8):
            sc.op("pe", lambda e, dc=dc: e.transpose(pjb[:, dc * 128:(dc + 1) * 128],
                                                     xn[:, dc * 128:(dc + 1) * 128], ident),
                  rxn + (r_cm,), (rpj,))
        sc.op("dve", lambda e: e.tensor_tensor(
            out=hT[:, :, b * 128:(b + 1) * 128],
            in0=pjb[:, 0:1024].rearrange("p (c t) -> p c t", c=8),
            in1=gcols.unsqueeze(2).to_broadcast([128, 8, 128]), op=ALU.mult),
              (rpj, r_par), (r_hT[b],))
        bfree(bi)

    def rmsnorm_to_hT(l):
        if l != layers[0]:
            return
        rmsnorm_prep(l)
        run_pipeline([[(lambda b=b: rms_a(l, b)), (lambda b=b: rms_b(l, b))] for b in range(16)])

    def proj_mm(wt, rw, u, c, pj, rpj, M=128, wcols=None):
        for dc in range(8):
            if wcols is None:
                lhs = wt[:, dc * 512 + u * 128: dc * 512 + u * 128 + M]
            else:
                lhs = wt[:, dc * wcols: dc * wcols + M]
            sc.op("pe", lambda e, dc=dc, lhs=lhs: e.matmul(pj[0:M, :], lhs, hT[:, dc, c * 512:(c + 1) * 512],
                                                            start=(dc == 0), stop=(dc == 7)),
                  (rw,) + tuple(r_hT[4 * c:4 * c + 4]), (rpj,))

    def gate_job(wt, rw, c, hp):
        def st0():
            pj, rpj, bi = balloc()
            proj_mm(wt, rw, 3, c, pj, rpj)
            sc.op("act", lambda e: e.activation(out=ogT[:, hp, c * 512:(c + 1) * 512], in_=pj[:], func=AF.Silu),
                  (rpj,), (r_og[hp][c],))
            bfree(bi)
        return [st0]

    def v_job(wt, rw, tokslices, g4):
        def st0():
            pj, rpj, bi = balloc()
            for s4 in range(4):
                sl = tokslices[4 * g4 + s4]
                for dc in range(8):
                    sc.op("pe", lambda e, dc=dc, s4=s4, sl=sl: e.matmul(
                        pj[:, s4 * 128:(s4 + 1) * 128], hT[:, dc, sl], wt[:, dc * 512 + 256: dc * 512 + 384],
                        start=(dc == 0), stop=(dc == 7)),
                          (rw,) + tuple(r_hT), (rpj,))
            sc.op("act", lambda e: e.copy(
                out=vt[:, 4 * g4:4 * g4 + 4, :].rearrange("p b (t c) -> p b t c", t=3)[:, :, 0:3:2, :],
                in_=pj[:].rearrange("p (b t c) -> p b t c", b=4, t=2)), (rpj,), (r_v[g4],))
            bfree(bi)
        return [st0]

    def qk_job(wt, rw, u, c, gcol, rg, dests, rot):
        n = jobn[0]
        jobn[0] += 1
        sq, rsq = sqT[n % 2], r_sq[n % 2]
        rs, rrs = rsT[n % 2], r_rs[n % 2]
        t1, rt1 = t1T[n % 2], r_t1[n % 2]
        cs = slice(c * 512, (c + 1) * 512)
        st = {}
        qnT, rqn = qnTs[n % 2], r_qn[n % 2]

        def st0():
            pj, rpj, bi = balloc()
            st["p"] = (pj, rpj, bi)
            proj_mm(wt, rw, u, c, pj, rpj)
            sc.op("act", lambda e: e.activation(out=sq[:], in_=pj[:], func=AF.Square), (rpj,), (rsq,))

        def st1():
            pj, rpj, bi = st["p"]
            pq, rpq, qi = balloc()
            sc.op("pe", lambda e: e.matmul(pq[:], bo64, sq[:], start=True, stop=True), (rsq, r_cm), (rpq,))
            sc.op("act", lambda e: e.activation(out=rs[:], in_=pq[:], func=AF.Ln, bias=epsc), (rpq, r_par), (rrs,))
            bfree(qi)
            sc.op("act", lambda e: e.activation(out=rs[:], in_=rs[:], func=AF.Exp, scale=-0.5), (rrs,), (rrs,))
            for (dt_, lo, hi, rd) in dests:
                sc.op("dve", lambda e, dt_=dt_, lo=lo, hi=hi: e.scalar_tensor_tensor(
                    out=dt_[lo:hi, cs], in0=pj[lo:hi, :], scalar=gcol[lo:hi, :], in1=rs[lo:hi, :],
                    op0=ALU.mult, op1=ALU.mult), (rpj, rg, rrs), (rd[c],))
            bfree(bi)

        def st1_full():
            pj, rpj, bi = st["p"]
            pq, rpq, qi = balloc()
            sc.op("pe", lambda e: e.matmul(pq[:], bo64, sq[:], start=True, stop=True), (rsq, r_cm), (rpq,))
            sc.op("act", lambda e: e.activation(out=rs[:], in_=pq[:], func=AF.Ln, bias=epsc), (rpq, r_par), (rrs,))
            bfree(qi)
            sc.op("act", lambda e: e.activation(out=rs[:], in_=rs[:], func=AF.Exp, scale=-0.5), (rrs,), (rrs,))
            sc.op("dve", lambda e: e.scalar_tensor_tensor(
                out=qnT, in0=pj[:], scalar=gcol, in1=rs[:], op0=ALU.mult, op1=ALU.mult),
                  (rpj, rg, rrs), (rqn,))
            bfree(bi)

        def st2_full():
            pr, rpr, ri = balloc()
            sc.op("pe", lambda e: e.matmul(pr[:], rmat, qnT, start=True, stop=True), (r_cm, rqn), (rpr,))
            sc.op("dve", lambda e: e.tensor_tensor(out=t1T[0], in0=pr[:], in1=tE[:, cs], op=ALU.mult),
                  (rpr, r_E[c]), (r_G2[0],))
            bfree(ri)
            sc.op("dve", lambda e: e.tensor_tensor(out=qnT, in0=qnT, in1=tD[:, cs], op=ALU.mult),
                  (rqn, r_D[c]), (rqn,))
            for (dt_, lo, hi, rd) in dests:
                sc.op("dve", lambda e, dt_=dt_, lo=lo, hi=hi: e.tensor_tensor(
                    out=dt_[lo:hi, cs], in0=qnT[lo:hi, :], in1=t1T[0][lo:hi, :], op=ALU.add),
                      (rqn, r_G2[0]), (rd[c],))

        if rot and len(dests) > 1:
            return [st0, st1_full, st2_full]

        def st2():
            pr, rpr, ri = balloc()
            for i, (dt_, lo, hi, rd) in enumerate(dests):
                sc.op("pe", lambda e, dt_=dt_, i=i: e.matmul(pr[:], rmat, dt_[:, cs], start=(i == 0),
                                                             stop=(i == len(dests) - 1)),
                      (r_cm, rd[c]), (rpr,))
            sc.op("dve", lambda e: e.tensor_tensor(out=t1, in0=pr[:], in1=tE[:, cs], op=ALU.mult),
                  (rpr, r_E[c]), (rt1,))
            bfree(ri)
            for (dt_, lo, hi, rd) in dests:
                sc.op("dve", lambda e, dt_=dt_, lo=lo, hi=hi: e.tensor_tensor(
                    out=dt_[lo:hi, cs], in0=dt_[lo:hi, cs], in1=tD[lo:hi, cs], op=ALU.mult),
                      (rd[c], r_D[c]), (rd[c],))
                sc.op("dve", lambda e, dt_=dt_, lo=lo, hi=hi: e.tensor_tensor(
                    out=dt_[lo:hi, cs], in0=dt_[lo:hi, cs], in1=t1[lo:hi, :], op=ALU.add),
                      (rd[c], rt1), (rd[c],))

        return [st0, st1, st2] if rot else [st0, st1]

    def epilogue(hp, c, sa, sb_, rA, rB, on_dve=False):
        cs = slice(c * 512, (c + 1) * 512)
        if on_dve:
            sc.op("dve", lambda e: e.reciprocal(out=G[0][0:64, :], in_=sa[64:128, :]), rA, (r_G[0],))
            sc.op("dve", lambda e: e.reciprocal(out=G[0][64:128, :], in_=sb_[0:64, :]), rB, (r_G[0],))
        else:
            sc.op("act", lambda e: e.activation(out=G[0][0:64, :], in_=sa[64:128, :], func=AF.Ln), rA, (r_G[0],))
            sc.op("act", lambda e: e.activation(out=G[0][64:128, :], in_=sb_[0:64, :], func=AF.Ln), rB, (r_G[0],))
            sc.op("act", lambda e: e.activation(out=G[0][:], in_=G[0][:], func=AF.Exp, scale=-1.0),
                  (r_G[0],), (r_G[0],))
        sc.op("dve", lambda e: e.tensor_tensor(out=G[2][0:64, :], in0=sa[0:64, :], in1=G[0][0:64, :], op=ALU.mult),
              tuple(rA) + (r_G[0],), tuple(r_G2))
        sc.op("dve", lambda e: e.tensor_tensor(out=G[2][64:128, :], in0=sb_[64:128, :], in1=G[0][64:128, :], op=ALU.mult),
              tuple(rB) + (r_G[0],), tuple(r_G2))
        sc.op("dve", lambda e: e.tensor_tensor(out=ogT[:, hp, cs], in0=G[2][:], in1=ogT[:, hp, cs], op=ALU.mult),
              tuple(r_G2) + (r_og[hp][c],), (r_og[hp][c],))

    def out_proj(l):
        li = layers.index(l)
        nxt = layers[li + 1] if li + 1 < len(layers) else None
        ws = [load_w(wout_d[l, nh, :, :]) for nh in range(2)]
        if nxt is not None:
            rmsnorm_prep(nxt)
            if nxt % 2 == 1:
                pos_prefetch()

        def out_block(tb):
            for nh in range(2):
                wt, rw = ws[nh]
                pj, rpj, bi = balloc()
                for ic in range(8):
                    sc.op("pe", lambda e, ic=ic, pj=pj, wt=wt: e.matmul(
                        pj[:], ogT[:, ic, tb * 128:(tb + 1) * 128], wt[:, ic * 512:(ic + 1) * 512],
                        start=(ic == 0), stop=(ic == 7)),
                          (rw, r_og[ic][tb // 4]), (rpj,))
                sc.op("dve", lambda e, nh=nh, pj=pj: e.tensor_tensor(
                    out=xres[:, tb, nh * 512:(nh + 1) * 512], in0=pj[:],
                    in1=xres[:, tb, nh * 512:(nh + 1) * 512], op=ALU.add),
                      (rpj, r_x[tb]), (r_x[tb],))
                bfree(bi)

        jobs = []
        for tb in range(16):
            j_ = [lambda tb=tb: out_block(tb)]
            if nxt is not None:
                j_ += [(lambda tb=tb: rms_a(nxt, tb)), (lambda tb=tb: rms_b(nxt, tb))]
            jobs.append(j_)
        run_pipeline(jobs)

    def fox_layer(l):
        j = l // 2
        pb = l * PW
        rmsnorm_to_hT(l)
        sc.op("dve", lambda e: e.memset(tC[64:128, :], 0.0), (), tuple(r_C))
        sc.op("dve", lambda e: e.memset(tD[0:64, :], 0.0), (), tuple(r_D))
        sc.op("dve", lambda e: e.memset(tA[64:128, :], 0.0), (), tuple(r_A))
        sc.op("dve", lambda e: e.memset(tB[0:64, :], 0.0), (), tuple(r_B))
        sc.op("dve", lambda e: e.tensor_scalar(out=sm[:, 32:33], in0=par[:, pb + 8:pb + 9], scalar1=0.125,
                                               scalar2=None, op0=ALU.mult), (r_par,), (r_sm,))
        wt, rw = load_w(wff_d[j, :, :], 768)
        for c in range(4):
            pj, rpj, bi = balloc()
            proj_mm(wt, rw, 0, c, pj, rpj, M=96, wcols=96)
            cs = slice(c * 512, (c + 1) * 512)
            sc.op("dve", lambda e, pj=pj, cs=cs: e.tensor_scalar(out=f1[0:96, cs], in0=pj[0:96, :],
                                                                 scalar1=par[0:96, pb + 10:pb + 11], scalar2=None,
                                                                 op0=ALU.add), (rpj, r_par), (r_f1[c],))
            bfree(bi)
            sc.op("act", lambda e, cs=cs: e.activation(out=f1[0:96, cs], in_=f1[0:96, cs], func=AF.Exp, scale=-1.0),
                  (r_f1[c],), (r_f1[c],))
            sc.op("act", lambda e, cs=cs: e.activation(out=f1[0:96, cs], in_=f1[0:96, cs], func=AF.Ln, bias=1.0),
                  (r_f1[c],), (r_f1[c],))
        for c in range(4):
            cs = slice(c * 512, (c + 1) * 512)
            init = 0.0 if c == 0 else f1[0:96, c * 512 - 1:c * 512]
            sc.op("dve", lambda e, cs=cs, init=init: e.tensor_tensor_scan(
                out=f1[0:96, cs], data0=onec[0:96, :].to_broadcast([96, 512]),
                data1=f1[0:96, cs], initial=init, op0=ALU.mult, op1=ALU.add),
                  (r_f1[c], r_par) + ((r_f1[c - 1],) if c else ()), (r_f1[c],))
        sc.op("dve", lambda e: e.memset(tE[96:97, :], 1.0), (), tuple(r_E))
        for c in range(4):
            cs = slice(c * 512, (c + 1) * 512)
            sc.op("dve", lambda e, cs=cs: e.tensor_copy(out=H[0][0:96, :], in_=f1[0:96, cs]), (r_f1[c],), (r_H[0],))
            sc.op("dve", lambda e, cs=cs: e.tensor_tensor(out=G[0][0:96, :], in0=f1[0:96, cs], in1=H[0][0:96, :],
                                                          op=ALU.subtract), (r_f1[c], r_H[0]), (r_G[0],))
            sc.op("dve", lambda e: e.tensor_copy(out=H[1][0:96, :], in_=G[0][0:96, :]), (r_G[0],), (r_H[1],))
            sc.op("dve", lambda e: e.tensor_tensor(out=G[1][0:96, :], in0=G[0][0:96, :], in1=H[1][0:96, :],
                                                   op=ALU.subtract), (r_G[0], r_H[1]), (r_G[1],))
            sc.op("dve", lambda e, cs=cs: e.tensor_copy(out=tE[0:16, cs], in_=H[0][0:16, :]), (r_H[0],), (r_E[c],))
            sc.op("dve", lambda e, cs=cs: e.tensor_copy(out=tE[32:48, cs], in_=H[1][32:48, :]), (r_H[1],), (r_E[c],))
            sc.op("dve", lambda e, cs=cs: e.tensor_copy(out=tE[64:80, cs], in_=G[1][64:80, :]), (r_G[1],), (r_E[c],))


        def aug_job(hp, side, c):
            sel = selm[:, side * 8 + hp, :]
            dA, dB = (tA, tB) if side == 0 else (tC, tD)
            rdA, rdB = (r_A, r_B) if side == 0 else (r_C, r_D)
            cs = slice(c * 512, (c + 1) * 512)

            def st0():
                pj, rpj, bi = balloc()
                sc.op("pe", lambda e: e.matmul(pj[0:70, :], sel, tE[:, cs], start=True, stop=True), (r_cm, r_E[c]), (rpj,))
                sc.op("dve", lambda e: e.tensor_copy(out=dA[64:70, cs], in_=pj[64:70, :]), (rpj,), (rdA[c],))
                sc.op("dve", lambda e: e.tensor_copy(out=dB[0:6, cs], in_=pj[0:6, :]), (rpj,), (rdB[c],))
                bfree(bi)
            return [st0]

        cidx = 0
        for hp in range(8):
            wt, rw = load_w(wfox_d[j, hp, :, :])
            jobs = [gate_job(wt, rw, c, hp) for c in range(4)]
            for side in (0, 1):
                for c in range(4):
                    jobs.append(aug_job(hp, side, c))
            for c in range(4):
                jobs.append(qk_job(wt, rw, 0, c, sm[:, 32:33], r_sm,
                                   [(tA, 0, 64, r_A), (tB, 64, 128, r_B)], False))
            for c in range(4):
                jobs.append(qk_job(wt, rw, 1, c, par[:, pb + 9:pb + 10], r_par,
                                   [(tC, 0, 64, r_C), (tD, 64, 128, r_D)], False))
            tks = [slice(b * 128, (b + 1) * 128) for b in range(16)]
            for g4 in range(4):
                jobs.append(v_job(wt, rw, tks, g4))
            run_pipeline(jobs)
            ul = [(balloc(), balloc()) for _ in range(2)]
            units = [(c, kb) for c in range(4) for kb in range(4 * c + 4)]
            pendq = []
            for ui in range(len(units) + 2):
                if ui < len(units):
                    c, kb = units[ui]
                    w = 512 if kb < 4 * c else (4 * c + 4 - kb) * 128
                    q0 = 512 * (c + 1) - w
                    sa, rsa, sai = balloc()
                    sbk, rsb, sbi = balloc()
                    pa, pb_ = ptall[:, (2 * ui) % 6, :], ptall[:, (2 * ui + 1) % 6, :]
                    rpa, rpb = r_pt[(2 * ui) % 6], r_pt[(2 * ui + 1) % 6]
                    ks = slice(kb * 128, (kb + 1) * 128)
                    qs = slice(q0, q0 + w)
                    qch = chunks_of(q0, q0 + w)
                    diag = kb >= 4 * c
                    for (st_, mv_, rst, rmv, sbank, rsbank) in ((tC, tA, r_C, r_A, sa, rsa), (tD, tB, r_D, r_B, sbk, rsb)):
                        sc.op("pe", lambda e, st_=st_, mv_=mv_, sbank=sbank, ks=ks, qs=qs, w=w, diag=diag: e.matmul(
                            sbank[:, 0:w], st_[:, ks], mv_[:, qs], start=True, stop=not diag),
                              (rst[kb // 4],) + tuple(rmv[i] for i in qch), (rsbank,))
                        if diag:
                            sc.op("pe", lambda e, sbank=sbank: e.matmul(sbank[:, 0:128], ident, mcur,
                                                                        start=False, stop=True),
                                  (r_cm,), (rsbank,))
                    sc.op("act", lambda e, pa=pa, sa=sa, w=w: e.activation(out=pa[:, 0:w], in_=sa[:, 0:w], func=AF.Exp),
                          (rsa,), (rpa,))
                    sc.op("act", lambda e, pb_=pb_, sbk=sbk, w=w: e.activation(out=pb_[:, 0:w], in_=sbk[:, 0:w], func=AF.Exp),
                          (rsb,), (rpb,))
                    bfree(sai)
                    bfree(sbi)
                    pendq.append((c, kb, w, pa, pb_, rpa, rpb))
                if ui >= 2 and pendq:
                    c_, kb_, w_, pa_, pbb_, rpa_, rpb_ = pendq.pop(0)
                    (UB, r_UB, _u), (LB, r_LB, _l) = ul[c_ % 2]
                    nkb = 4 * c_ + 4
                    first = kb_ == 0
                    last = kb_ == nkb - 1
                    osl = slice(512 - w_, 512)
                    rl = r_v[kb_ // 4]
                    sc.op("pe", lambda e, UB=UB, kb_=kb_, pa_=pa_, osl=osl, w_=w_, first=first, last=last: e.matmul(
                        UB[:, osl], vt[:, kb_, 0:128], pa_[:, 0:w_], start=first, stop=last, skip_group_check=True),
                          (rl, rpa_), (r_UB,))
                    sc.op("pe", lambda e, LB=LB, kb_=kb_, pbb_=pbb_, osl=osl, w_=w_, first=first, last=last: e.matmul(
                        LB[:, osl], vt[:, kb_, 64:192], pbb_[:, 0:w_], start=first, stop=last, skip_group_check=True),
                          (rl, rpb_), (r_LB,))
                    if last:
                        epilogue(hp, c_, UB, LB, (r_UB,), (r_LB,), on_dve=True)
            for (a_, b_) in ul:
                bfree(a_[2])
                bfree(b_[2])
        out_proj(l)

    def pos_prefetch():
        sc.dma("sp", "posall", lambda e: e.dma_start(out=f1[:].bitcast(I32), in_=pos_d[:, :]), (), tuple(r_f1))

    def rotary_tables():
        for c in range(4):
            cs = slice(c * 512, (c + 1) * 512)
            gi = G[0][:].bitcast(I32)
            sc.op("dve", lambda e, cs=cs: e.tensor_copy(out=G[1][:], in_=f1[:].bitcast(I32)[:, cs]),
                  (r_f1[c],), (r_G[1],))
            for which in (0, 1):
                dst, rdst = (tD, r_D) if which == 0 else (tE, r_E)
                sc.op("dve", lambda e, which=which: e.tensor_scalar(
                    out=G[2][:], in0=G[1][:], scalar1=cpar[:, 0:1], scalar2=(0.25 if which == 0 else 0.0),
                    op0=ALU.mult, op1=ALU.add), (r_G[1], r_par), tuple(r_G2))
                sc.op("dve", lambda e, gi=gi: e.tensor_copy(out=gi, in_=G[2][:]), tuple(r_G2), (r_G[0],))
                sc.op("dve", lambda e, gi=gi: e.tensor_copy(out=f2[:, 0:512], in_=gi), (r_G[0],), (r_f2[0],))
                sc.op("dve", lambda e: e.tensor_tensor(out=G[2][:], in0=G[2][:], in1=f2[:, 0:512], op=ALU.subtract),
                      tuple(r_G2) + (r_f2[0],), tuple(r_G2))
                if which == 0:
                    sc.op("act", lambda e, dst=dst, cs=cs: e.activation(out=dst[:, cs], in_=G[2][:], func=AF.Sin,
                                                                         scale=float(2 * np.pi)),
                          tuple(r_G2), (rdst[c],))
                else:
                    sc.op("act", lambda e, dst=dst, cs=cs: e.activation(out=dst[:, cs], in_=G[2][:], func=AF.Sin,
                                                                         scale=cpar[:, 1:2]),
                          tuple(r_G2) + (r_par,), (rdst[c],))

    def dsw_layer(l):
        j = l // 2
        pb = l * PW
        rmsnorm_to_hT(l)
        if l == layers[0]:
            pos_prefetch()
        rotary_tables()
        sc.op("dve", lambda e: e.memset(tA[64:128, :], 0.0), (), tuple(r_A))
        sc.op("dve", lambda e: e.memset(tB[0:64, :], 0.0), (), tuple(r_B))
        sc.op("dve", lambda e: e.tensor_scalar(out=sm[:, 33:36], in0=par[:, pb + 8:pb + 11], scalar1=0.125,
                                               scalar2=None, op0=ALU.mult), (r_par,), (r_sm,))
        for i_, m_ in enumerate((3, 3, 4, 4)):
            sc.op("dve", lambda e, i_=i_, m_=m_: e.tensor_copy(out=ptall[:, 3, i_ * 128:(i_ + 1) * 128], in_=cmat[:, m_, :]),
                  (r_cm,), (r_pt[3],))
        qidx = 0
        for hp in range(8):
            for g, (window, r) in enumerate(DSWA):
                nb = 16 // r
                wt, rw = load_w(wdsw_d[j, hp * 3 + g, :, :])

                def tsl(rho, jb, r=r):
                    st0 = rho + r * 128 * jb
                    return slice(st0, st0 + 127 * r + 1, r) if r > 1 else slice(st0, st0 + 128)

                blks = [(rho, jb) for rho in range(r) for jb in range(nb)]
                jobs = []
                if g == 0:
                    jobs += [gate_job(wt, rw, c, hp) for c in range(4)]
                for c in range(4):
                    jobs.append(qk_job(wt, rw, 0, c, sm[:, 33 + g:34 + g], r_sm,
                                       [(tA, 0, 64, r_A), (tB, 64, 128, r_B)], True))
                for c in range(4):
                    jobs.append(qk_job(wt, rw, 1, c, par[:, pb + 11 + g:pb + 12 + g], r_par,
                                       [(tC, 0, 128, r_C)], True))
                tks = [tsl(rho, jb) for (rho, jb) in blks]
                for g4 in range(4):
                    jobs.append(v_job(wt, rw, tks, g4))
                run_pipeline(jobs)

                ul = [(balloc(), balloc()) for _ in range(2)]
                ajobs = []
                for bi_ in range(16):
                    def mk(bi_=bi_, qd0=qidx):
                        qd, s4 = bi_ // 4, bi_ % 4
                        rho, jb = blks[bi_]
                        keys = [bi_] + ([bi_ - 1] if jb > 0 else [])
                        ncol = 256 * len(keys)
                        pt, rpt = ptall[:, bi_ % 3, :], r_pt[bi_ % 3]
                        qsl = tsl(rho, jb)
                        (UB, r_UB, _u), (LB, r_LB, _l) = ul[(qd0 + qd) % 2]

                        def st0():
                            sbank, rsbank, si = balloc()
                            first_mm = True
                            for ki, kblk in enumerate(keys):
                                ksl = tsl(*blks[kblk])
                                for h, (mv_, rmv) in enumerate(((tA, r_A), (tB, r_B))):
                                    col = (2 * ki + h) * 128
                                    sc.op("pe", lambda e, col=col, ksl=ksl, mv_=mv_, fm=first_mm: e.matmul(
                                        sbank[:, col:col + 128], tC[:, ksl], mv_[:, qsl], start=fm, stop=False,
                                        skip_group_check=True),
                                          tuple(r_C) + tuple(rmv), (rsbank,))
                                    first_mm = False
                            sc.op("pe", lambda e: e.matmul(sbank[:, 0:ncol], ident, ptall[:, 3, 0:ncol], start=False,
                                                           stop=True, skip_group_check=True),
                                  (r_cm, r_pt[3]), (rsbank,))
                            sc.op("act", lambda e: e.activation(out=pt[:, 0:ncol], in_=sbank[:, 0:ncol], func=AF.Exp),
                                  (rsbank,), (rpt,))
                            bfree(si)

                        def st1():
                            for ki, kblk in enumerate(keys):
                                rl = r_v[kblk // 4]
                                fs = (s4 == 0 and ki == 0)
                                sc.op("pe", lambda e, ki=ki, kblk=kblk, fs=fs: e.matmul(
                                    UB[:, s4 * 128:(s4 + 1) * 128], vt[:, kblk, 0:128], pt[:, (2 * ki) * 128:(2 * ki + 1) * 128],
                                    start=fs, stop=False, skip_group_check=True), (rl, rpt), (r_UB,))
                                sc.op("pe", lambda e, ki=ki, kblk=kblk, fs=fs: e.matmul(
                                    LB[:, s4 * 128:(s4 + 1) * 128], vt[:, kblk, 64:192], pt[:, (2 * ki + 1) * 128:(2 * ki + 2) * 128],
                                    start=fs, stop=False, skip_group_check=True), (rl, rpt), (r_LB,))
                            if s4 == 3:
                                if r == 1:
                                    av = lambda t: t[:, qd * 512:(qd + 1) * 512]
                                    rch = (qd,)
                                    pv = lambda b: b[:]
                                elif r == 4:
                                    av = lambda t: t[:, qd:S:4]
                                    rch = (0, 1, 2, 3)
                                    pv = lambda b: b[:]
                                else:
                                    av = lambda t: t[:].rearrange("p (i r) -> p r i", r=16)[:, 4 * qd:4 * qd + 4, :]
                                    rch = (0, 1, 2, 3)
                                    pv = lambda b: b[:].rearrange("p (s i) -> p s i", s=4)
                                for (bank, rbank, acc, racc) in ((UB, r_UB, f1, r_f1), (LB, r_LB, f2, r_f2)):
                                    if g == 0:
                                        sc.op("act", lambda e, bank=bank, acc=acc: e.copy(out=av(acc), in_=pv(bank)),
                                              (rbank,), tuple(racc[i] for i in rch))
                                    else:
                                        sc.op("dve", lambda e, bank=bank, acc=acc: e.tensor_tensor(
                                            out=av(acc), in0=pv(bank), in1=av(acc), op=ALU.add),
                                              (rbank,) + tuple(racc[i] for i in rch), tuple(racc[i] for i in rch))
                        return [st0, (lambda: None), st1]
                    ajobs.append(mk())
                run_pipeline(ajobs)
                for (a_, b_) in ul:
                    bfree(a_[2])
                    bfree(b_[2])
                qidx += 4
            for c in range(4):
                cs = slice(c * 512, (c + 1) * 512)
                epilogue(hp, c, f1[:, cs], f2[:, cs], (r_f1[c],), (r_f2[c],))
        out_proj(l)

    for l in layers:
        if l % 2 == 0:
            fox_layer(l)
        else:
            dsw_layer(l)

    for b in range(16):
        sc.dma("sp", "o%d" % b,
               lambda e, b=b: e.dma_start(out=y_d[b * 128:(b + 1) * 128, :], in_=xres[:, b, :]),
               (r_x[b],), ())
    outres = []
    for b in range(16):
        rr = Res("oo%d" % b)
        rr.w = ("d", "o%d" % b, 1)
        outres.append(rr)
    sc.wait_all("sp", outres)
    sc.emit(nc, es)
    es.close()
    return nc


def _consts():
    cm = np.zeros((6, 128, 128), np.float32)
    cm[0] = np.eye(128)
    p = np.arange(128)
    cm[1] = (p[:, None] // 64 == p[None, :] // 64) / 64.0
    for base in (0, 64):
        for i in range(8):
            cm[2][base + i + 8, base + i] = 1.0
            cm[2][base + i, base + i + 8] = 1.0
    k = p[:, None]
    q = p[None, :]
    cm[3] = np.where(q >= k, 0.0, NEG)
    cm[4] = np.where(q <= k, 0.0, NEG)
    sel = np.zeros((16, 128, 128), np.float32)
    for hp in range(8):
        a, b = 2 * hp, 2 * hp + 1
        sq = sel[hp]
        sk = sel[8 + hp]
        for jj in range(3):
            sq[32 * jj + a, 64 + jj] = -1.0
            sq[96, 64 + 3 + jj] = 1.0
            sq[32 * jj + b, jj] = -1.0
            sq[96, 3 + jj] = 1.0
            sk[96, 64 + jj] = 1.0
            sk[32 * jj + a, 64 + 3 + jj] = 1.0
            sk[96, jj] = 1.0
            sk[32 * jj + b, 3 + jj] = 1.0
    cmat = np.ascontiguousarray(np.concatenate(
        [cm.transpose(1, 0, 2).reshape(128, 6 * 128), sel[:, :, 0:70].transpose(1, 0, 2).reshape(128, 16 * 70)], axis=1))
    inv_freq = (np.float32(500000.0) ** (-np.arange(0, 16, 2, dtype=np.float32) / np.float32(16))).astype(np.float32)
    cpar = np.zeros((128, 4), np.float32)
    for base in (0, 64):
        for i in range(8):
            cpar[base + i, 0] = inv_freq[i] / (2 * np.pi)
            cpar[base + i + 8, 0] = inv_freq[i] / (2 * np.pi)
            cpar[base + i, 1] = -2 * np.pi
            cpar[base + i + 8, 1] = 2 * np.pi
    return cmat, cpar


def _prep_shared(norm_g, fox_w_in, fox_b_f, fox_q_norm, fox_k_norm, fox_w_out,
                 dsw_w_in, dsw_q_norm, dsw_k_norm, dsw_w_out):
    f = np.float32
    par = np.zeros((128, DEPTH * PW), f)
    p = np.arange(128)
    for l in range(DEPTH):
        par[:, l * PW:l * PW + 8] = np.asarray(norm_g[l], f).reshape(8, 128).T
        j = l // 2
        if l % 2 == 0:
            par[:, l * PW + 8] = np.asarray(fox_q_norm[j], f)[p % 64]
            par[:, l * PW + 9] = np.asarray(fox_k_norm[j], f)[p % 64]
            for base in (0, 32, 64):
                par[base:base + 16, l * PW + 10] = np.asarray(fox_b_f[j], f)
        else:
            for g in range(3):
                par[:, l * PW + 8 + g] = np.asarray(dsw_q_norm[j, g], f)[p % 64]
                par[:, l * PW + 11 + g] = np.asarray(dsw_k_norm[j, g], f)[p % 64]
    par[:, DEPTH * PW - 1] = EPS
    par[:, DEPTH * PW - 2] = 1.0
    w = np.asarray(fox_w_in, f)
    w4 = w[:, :, :4096].reshape(2, 8, 128, 4, 8, 128)
    wfox = np.ascontiguousarray(w4.transpose(0, 4, 2, 1, 3, 5)).reshape(2, 8, 128, 4096)
    wf = w[:, :, 4096:4112].reshape(2, 8, 128, 16)
    wff = np.zeros((2, 128, 8, 96), f)
    for base in (0, 32, 64):
        wff[:, :, :, base:base + 16] = wf.transpose(0, 2, 1, 3)
    wff = wff.reshape(2, 128, 768)
    wd = np.asarray(dsw_w_in, f).reshape(2, 8, 128, 10240)
    wdsw = np.empty((2, 8, 3, 128, 8, 4, 128), f)
    for hp in range(8):
        for g in range(3):
            for u in range(3):
                c0 = u * 3072 + g * 1024 + hp * 128
                wdsw[:, hp, g, :, :, u, :] = wd[:, :, :, c0:c0 + 128].transpose(0, 2, 1, 3)
            c0 = 9216 + hp * 128
            wdsw[:, hp, g, :, :, 3, :] = wd[:, :, :, c0:c0 + 128].transpose(0, 2, 1, 3)
    wdsw = wdsw.reshape(2, 24, 128, 4096)
    wout = np.empty((4, 2, 128, 8, 512), f)
    for l in range(DEPTH):
        wo = np.asarray(fox_w_out[l // 2] if l % 2 == 0 else dsw_w_out[l // 2], f).reshape(8, 128, 2, 512)
        wout[l] = wo.transpose(2, 1, 0, 3)
    wout = wout.reshape(4, 2, 128, 4096)
    cmat, cpar = _consts()
    return {"par": par, "cpar": cpar, "cmat": cmat, "wfox": wfox, "wff": wff, "wdsw": wdsw, "wout": wout}


_PROG_CACHE = {}


def _get_prog(layers):
    key = tuple(layers)
    if key not in _PROG_CACHE:
        _PROG_CACHE[key] = build_program(list(layers))
    return _PROG_CACHE[key]


def kernel(x, positions, norm_g, fox_w_in, fox_b_f, fox_q_norm, fox_k_norm, fox_w_out,
           dsw_w_in, dsw_q_norm, dsw_k_norm, dsw_w_out, _groups=None):
    shared = _prep_shared(norm_g, fox_w_in, fox_b_f, fox_q_norm, fox_k_norm, fox_w_out,
                          dsw_w_in, dsw_q_norm, dsw_k_norm, dsw_w_out)
    x = np.asarray(x, np.float32)
    positions = np.asarray(positions, np.int32)
    B = x.shape[0]
    cur = [np.ascontiguousarray(x[b]) for b in range(B)]
    posb = [np.ascontiguousarray(np.broadcast_to(positions[b][None, :], (128, S))) for b in range(B)]
    for layers in (_groups or LAUNCH_GROUPS):
        nc = _get_prog(layers)
        in_maps = []
        for b in range(B):
            m = dict(shared)
            m["x"] = cur[b]
            m["pos"] = posb[b]
            in_maps.append(m)
        res = run_bass_kernel_spmd(nc, in_maps, core_ids=list(range(B)))
        cur = [np.asarray(res.results[b]["y"], np.float32) for b in range(B)]
    return np.stack(cur, axis=0)
```

```python
import numpy as np
from contextlib import ExitStack
import concourse.bass as bass
import concourse.mybir as mybir
from concourse.bass_utils import run_bass_kernel_spmd

F32 = mybir.dt.float32
BF16 = mybir.dt.bfloat16
I32 = mybir.dt.int32
AF = mybir.ActivationFunctionType
ALU = mybir.AluOpType

D = 1024
S = 2048
NH = 16
HD = 64
DEPTH = 4
EPS = 1e-6
NEG = -30000.0
ENGS = ("pe", "act", "dve", "pool", "sp")
LAUNCH_GROUPS = [[0, 1, 2, 3]]
DSWA = ((128, 1), (512, 4), (2048, 16))
PW = 16


class Res:
    __slots__ = ("name", "w", "r", "excl", "strict")

    def __init__(self, name, excl=False, strict=False):
        self.name = name
        self.w = None
        self.r = {}
        self.excl = excl
        self.strict = strict


class Sched:
    def __init__(self):
        self.ops = {e: [] for e in ENGS}
        self.known = {e: {} for e in ENGS}
        self.dcount = {}
        self.dsnap = {}

    def _collect(self, eng, reads, writes):
        deps = {}

        def add(tok, raw):
            if tok is None:
                return
            kind, key, idx = tok
            if kind == "e" and key == eng and (eng == "pe" or not raw):
                return
            k = (kind, key)
            if deps.get(k, -1) < idx:
                deps[k] = idx

        for r in reads:
            add(r.w, True)
            if r.excl:
                for k, idx in r.r.items():
                    if not (k[0] == "e" and k[1] == eng):
                        add((k[0], k[1], idx), False)
        for w in writes:
            add(w.w, w.strict)
            for k, idx in w.r.items():
                add((k[0], k[1], idx), w.strict)
        kn = self.known[eng]
        waits = []
        for k, idx in deps.items():
            if kn.get(k, -1) >= idx:
                continue
            waits.append((k, idx))
        for k, idx in waits:
            if kn.get(k, -1) < idx:
                kn[k] = idx
            snap = self.ops[k[1]][idx]["snap"] if k[0] == "e" else self.dsnap.get((k[1], idx), {})
            for kk, vv in snap.items():
                if kn.get(kk, -1) < vv:
                    kn[kk] = vv
            if k[0] == "e":
                self.ops[k[1]][idx]["sig"] = True
        return waits

    def op(self, eng, fn, reads=(), writes=()):
        waits = self._collect(eng, reads, writes)
        idx = len(self.ops[eng])
        self.ops[eng].append({"kind": "c", "fn": fn, "waits": waits, "sig": False,
                              "snap": dict(self.known[eng])})
        tok = ("e", eng)
        for r in reads:
            r.r[tok] = idx
        for w in writes:
            w.w = ("e", eng, idx)
            w.r = {}

    def dma(self, eng, key, fn, reads=(), writes=()):
        waits = self._collect(eng, reads, writes)
        cnt = self.dcount.get(key, 0) + 1
        self.dcount[key] = cnt
        self.ops[eng].append({"kind": "d", "fn": fn, "waits": waits, "sig": False, "dkey": key,
                              "snap": dict(self.known[eng])})
        self.dsnap[(key, cnt)] = dict(self.known[eng])
        tok = ("d", key)
        for r in reads:
            r.r[tok] = cnt
        for w in writes:
            w.w = ("d", key, cnt)
            w.r = {}

    def wait_all(self, eng, reslist):
        waits = self._collect(eng, reslist, ())
        self.ops[eng].append({"kind": "w", "fn": None, "waits": waits, "sig": False,
                              "snap": dict(self.known[eng])})

    def emit(self, nc, es):
        sems = {e: es.enter_context(nc.semaphore("s_" + e)) for e in ENGS}
        dsems = {k: es.enter_context(nc.semaphore("d_" + k)) for k in self.dcount}
        for e in ENGS:
            c = 0
            for o in self.ops[e]:
                if o["kind"] == "c" and o["sig"]:
                    c += 1
                    o["val"] = c
        ops = self.ops

        def run(e, eng):
            for o in ops[e]:
                wl = [(sems[k[1]], ops[k[1]][idx]["val"]) if k[0] == "e" else (dsems[k[1]], 16 * idx)
                      for k, idx in o["waits"]]
                attach = None
                if o["kind"] != "w" and wl:
                    attach = wl.pop()
                for sm_, v_ in wl:
                    eng.wait_ge(sm_, v_)
                if o["kind"] == "w":
                    continue
                ins = o["fn"](eng)
                if attach is not None:
                    ins._wait_ge(attach[0], attach[1])
                if o["kind"] == "d":
                    ins.then_inc(dsems[o["dkey"]], 16)
                elif o["sig"]:
                    ins.then_inc(sems[e], 1)

        with nc.Block() as block:
            @block.tensor
            def _(t):
                run("pe", t)

            @block.scalar
            def _(a):
                run("act", a)

            @block.vector
            def _(v):
                run("dve", v)

            @block.gpsimd
            def _(g):
                run("pool", g)

            @block.sync
            def _(s):
                run("sp", s)


def run_pipeline(jobs):
    n = len(jobs)
    if n == 0:
        return
    mx = max(len(j) for j in jobs)
    for t in range(n + mx - 1):
        for k in range(mx):
            i = t - k
            if 0 <= i < n and k < len(jobs[i]):
                jobs[i][k]()


def build_program(layers):
    nc = bass.Bass("TRN2", target_bir_lowering=False)
    es = ExitStack()
    sc = Sched()

    x_d = nc.dram_tensor("x", [S, D], F32, kind="ExternalInput").ap()
    y_d = nc.dram_tensor("y", [S, D], F32, kind="ExternalOutput").ap()
    pos_d = nc.dram_tensor("pos", [128, S], I32, kind="ExternalInput").ap()
    par_d = nc.dram_tensor("par", [128, DEPTH * PW], F32, kind="ExternalInput").ap()
    cpar_d = nc.dram_tensor("cpar", [128, 4], F32, kind="ExternalInput").ap()
    cmat_d = nc.dram_tensor("cmat", [128, 6 * 128 + 16 * 70], F32, kind="ExternalInput").ap()
    wfox_d = nc.dram_tensor("wfox", [2, 8, 128, 4096], F32, kind="ExternalInput").ap()
    wff_d = nc.dram_tensor("wff", [2, 128, 8 * 96], F32, kind="ExternalInput").ap()
    wdsw_d = nc.dram_tensor("wdsw", [2, 24, 128, 4096], F32, kind="ExternalInput").ap()
    wout_d = nc.dram_tensor("wout", [4, 2, 128, 4096], F32, kind="ExternalInput").ap()

    def sb(name, shape, dt):
        return es.enter_context(nc.sbuf_tensor(name, shape, dt))

    xres = sb("xres", [128, 16, D], F32)
    hT = sb("hT", [128, 8, S], BF16)
    ogT = sb("ogT", [128, 8, S], BF16)
    wsl = [sb("wsl%d" % i, [128, 4096], BF16) for i in range(2)]
    tA = sb("tA", [128, S], BF16)
    tB = sb("tB", [128, S], BF16)
    tC = sb("tC", [128, S], BF16)
    tD = sb("tD", [128, S], BF16)
    tE = sb("tE", [128, S], BF16)
    vt = sb("vt", [128, 16, 192], BF16)
    f1 = sb("f1", [128, S], F32)
    f2 = sb("f2", [128, S], F32)
    cmat_f = sb("cmat_s", [128, 6 * 128 + 16 * 70], BF16)
    cmat = cmat_f[:, 0:768].rearrange("p (a b) -> p a b", a=6)
    selm = cmat_f[:, 768:768 + 1120].rearrange("p (a b) -> p a b", a=16)
    ptall = sb("ptall", [128, 6, 512], BF16)
    G = [sb("g%d" % i, [128, 512], F32) for i in range(3)]
    H = [sb("h%d" % i, [128, 512], BF16) for i in range(3)]
    par = sb("par_s", [128, DEPTH * PW], F32)
    cpar = sb("cpar_s", [128, 4], F32)
    sm = sb("sm", [128, 48], F32)

    banks = [es.enter_context(nc.psum_tensor("bk%d" % i, [128, 512], F32)) for i in range(8)]
    free_list = list(range(8))

    R = {}

    def res(name, excl=False):
        if name not in R:
            R[name] = Res(name, excl)
        return R[name]

    r_bk = [res("BK%d" % i, True) for i in range(8)]

    def balloc():
        i = free_list.pop(0)
        return banks[i], r_bk[i], i

    def bfree(i):
        free_list.append(i)

    r_x = [res("x%d" % b) for b in range(16)]
    r_hT = [res("hT%d" % b) for b in range(16)]
    r_og = [[res("og%d_%d" % (h_, c)) for c in range(4)] for h_ in range(8)]
    r_w = [res("w%d" % i) for i in range(2)]
    r_A = [res("A%d" % c) for c in range(4)]
    r_B = [res("B%d" % c) for c in range(4)]
    r_C = [res("C%d" % c) for c in range(4)]
    r_D = [res("D%d" % c) for c in range(4)]
    r_E = [res("E%d" % c) for c in range(4)]
    r_v = [res("v%d" % c) for c in range(4)]
    r_f1 = [res("f1_%d" % c) for c in range(4)]
    r_f2 = [res("f2_%d" % c) for c in range(4)]
    r_cm = res("cmat")
    r_pt = [res("pt%d" % i) for i in range(6)]
    r_G = [res("G%d" % i) for i in range(2)]
    r_G2 = [res("G2a"), res("G2b")]
    r_H = [res("H%d" % i) for i in range(3)]
    r_par = res("par")
    r_sm = res("sm")
    r_one = res("ones64")
    r_junk = res("junk")
    r_junk.strict = True

    ident = cmat[:, 0, :]
    bo64 = cmat[:, 1, :]
    rmat = cmat[:, 2, :]
    mcur = cmat[:, 3, :]
    epsc = par[:, DEPTH * PW - 1:DEPTH * PW]
    onec = par[:, DEPTH * PW - 2:DEPTH * PW - 1]

    sqT = [H[0], H[1]]
    r_sq = [r_H[0], r_H[1]]
    rsT = [G[0], G[1]]
    r_rs = [r_G[0], r_G[1]]
    g2b = G[2][:].bitcast(BF16)
    t1T = [g2b[:, 0:512], g2b[:, 0:512]]
    r_t1 = [r_G2[0], r_G2[0]]
    qnTs = [g2b[:, 512:1024], H[2][:, :]]
    r_qn = [r_G2[1], r_H[2]]
    jobn = [0]

    def chunks_of(lo, hi):
        return list(range(lo // 512, (hi - 1) // 512 + 1))

    sc.dma("pool", "cm", lambda e: e.dma_start(out=cmat_f[:], in_=cmat_d[:, :]),
           (), (r_cm,))
    sc.dma("sp", "par", lambda e: e.dma_start(out=par[:], in_=par_d[:, :]), (), (r_par,))
    sc.dma("sp", "par", lambda e: e.dma_start(out=cpar[:], in_=cpar_d[:, :]), (), (r_par,))
    for b in range(16):
        sc.dma("sp", "x%d" % b,
               lambda e, b=b: e.dma_start(out=xres[:, b, :], in_=x_d[b * 128:(b + 1) * 128, :]),
               (), (r_x[b],))
    sc.op("dve", lambda e: e.memset(vt[:, :, 64:128], 1.0), (), tuple(res("v%d" % c) for c in range(4)))
    for t_, r_ in ((tA, r_A), (tB, r_B), (tC, r_C), (tD, r_D), (tE, r_E)):
        sc.op("dve", lambda e, t_=t_: e.memset(t_[:], 0.0), (), tuple(r_))

    wq = {"n": 0}

    def load_w(dram_ap, ncols=4096):
        i = wq["n"] % 2
        wq["n"] += 1
        sc.dma("pool", "w%d" % i,
               lambda e, i=i, a=dram_ap, n=ncols: e.dma_start(out=wsl[i][:, 0:n], in_=a),
               (), (r_w[i],))
        return wsl[i], r_w[i]

    def rmsnorm_prep(l):
        sc.op("dve", lambda e: e.memset(sm[:, 0:16], 0.0), (), (r_sm,))

    def rms_a(l, b):
        ptf = ptall[:].rearrange("p a b -> p (a b)")
        xn = (ptf[:, 0:1024], ptf[:, 1024:2048])[b % 2]
        rxn = ((r_pt[0], r_pt[1]), (r_pt[2], r_pt[3]))[b % 2]
        junk = f2[:].bitcast(BF16)[:, 0:1024]
        sc.op("act", lambda e: e.activation(out=junk, in_=xres[:, b, :], func=AF.Square,
                                            accum_out=sm[:, b:b + 1]), (r_x[b],), (r_f2[0], r_junk, r_sm))
        sc.op("act", lambda e: e.activation(out=sm[:, 16 + b:17 + b], in_=sm[:, b:b + 1], func=AF.Ln,
                                            scale=1.0 / D, bias=epsc), (r_sm, r_par), (r_sm,))
        sc.op("act", lambda e: e.activation(out=sm[:, 16 + b:17 + b], in_=sm[:, 16 + b:17 + b], func=AF.Exp,
                                            scale=-0.5), (r_sm,), (r_sm,))
        sc.op("dve", lambda e: e.tensor_scalar(out=xn, in0=xres[:, b, :], scalar1=sm[:, 16 + b:17 + b],
                                               scalar2=None, op0=ALU.mult), (r_x[b], r_sm), rxn)

    def rms_b(l, b):
        gcols = par[:, l * PW:l * PW + 8]
        ptf = ptall[:].rearrange("p a b -> p (a b)")
        xn = (ptf[:, 0:1024], ptf[:, 1024:2048])[b % 2]
        rxn = ((r_pt[0], r_pt[1]), (r_pt[2], r_pt[3]))[b % 2]
        pj, rpj, bi = balloc()
        pjb = pj[:].bitcast(BF16)
        for dc in range(8):
            sc.op("pe", lambda e, dc=dc: e.transpose(pjb[:, dc * 128:(dc + 1) * 128],
                                                     xn[:, dc * 128:(dc + 1) * 128], ident),
                  rxn + (r_cm,), (rpj,))
        sc.op("dve", lambda e: e.tensor_tensor(
            out=hT[:, :, b * 128:(b + 1) * 128],
            in0=pjb[:, 0:1024].rearrange("p (c t) -> p c t", c=8),
            in1=gcols.unsqueeze(2).to_broadcast([128, 8, 128]), op=ALU.mult),
              (rpj, r_par), (r_hT[b],))
        bfree(bi)

    def rmsnorm_to_hT(l):
        if l != layers[0]:
            return
        rmsnorm_prep(l)
        run_pipeline([[(lambda b=b: rms_a(l, b)), (lambda b=b: rms_b(l, b))] for b in range(16)])

    def proj_mm(wt, rw, u, c, pj, rpj, M=128, wcols=None):
        for dc in range(8):
            if wcols is None:
                lhs = wt[:, dc * 512 + u * 128: dc * 512 + u * 128 + M]
            else:
                lhs = wt[:, dc * wcols: dc * wcols + M]
            sc.op("pe", lambda e, dc=dc, lhs=lhs: e.matmul(pj[0:M, :], lhs, hT[:, dc, c * 512:(c + 1) * 512],
                                                            start=(dc == 0), stop=(dc == 7)),
                  (rw,) + tuple(r_hT[4 * c:4 * c + 4]), (rpj,))

    def gate_job(wt, rw, c, hp):
        def st0():
            pj, rpj, bi = balloc()
            proj_mm(wt, rw, 3, c, pj, rpj)
            sc.op("act", lambda e: e.activation(out=ogT[:, hp, c * 512:(c + 1) * 512], in_=pj[:], func=AF.Silu),
                  (rpj,), (r_og[hp][c],))
            bfree(bi)
        return [st0]

    def v_job(wt, rw, tokslices, g4):
        def st0():
            pj, rpj, bi = balloc()
            for s4 in range(4):
                sl = tokslices[4 * g4 + s4]
                for dc in range(8):
                    sc.op("pe", lambda e, dc=dc, s4=s4, sl=sl: e.matmul(
                        pj[:, s4 * 128:(s4 + 1) * 128], hT[:, dc, sl], wt[:, dc * 512 + 256: dc * 512 + 384],
                        start=(dc == 0), stop=(dc == 7)),
                          (rw,) + tuple(r_hT), (rpj,))
            sc.op("act", lambda e: e.copy(
                out=vt[:, 4 * g4:4 * g4 + 4, :].rearrange("p b (t c) -> p b t c", t=3)[:, :, 0:3:2, :],
                in_=pj[:].rearrange("p (b t c) -> p b t c", b=4, t=2)), (rpj,), (r_v[g4],))
            bfree(bi)
        return [st0]

    def qk_job(wt, rw, u, c, gcol, rg, dests, rot):
        n = jobn[0]
        jobn[0] += 1
        sq, rsq = sqT[n % 2], r_sq[n % 2]
        rs, rrs = rsT[n % 2], r_rs[n % 2]
        t1, rt1 = t1T[n % 2], r_t1[n % 2]
        cs = slice(c * 512, (c + 1) * 512)
        st = {}
        qnT, rqn = qnTs[n % 2], r_qn[n % 2]

        def st0():
            pj, rpj, bi = balloc()
            st["p"] = (pj, rpj, bi)
            proj_mm(wt, rw, u, c, pj, rpj)
            sc.op("act", lambda e: e.activation(out=sq[:], in_=pj[:], func=AF.Square), (rpj,), (rsq,))

        def st1():
            pj, rpj, bi = st["p"]
            pq, rpq, qi = balloc()
            sc.op("pe", lambda e: e.matmul(pq[:], bo64, sq[:], start=True, stop=True), (rsq, r_cm), (rpq,))
            sc.op("act", lambda e: e.activation(out=rs[:], in_=pq[:], func=AF.Ln, bias=epsc), (rpq, r_par), (rrs,))
            bfree(qi)
            sc.op("act", lambda e: e.activation(out=rs[:], in_=rs[:], func=AF.Exp, scale=-0.5), (rrs,), (rrs,))
            for (dt_, lo, hi, rd) in dests:
                sc.op("dve", lambda e, dt_=dt_, lo=lo, hi=hi: e.scalar_tensor_tensor(
                    out=dt_[lo:hi, cs], in0=pj[lo:hi, :], scalar=gcol[lo:hi, :], in1=rs[lo:hi, :],
                    op0=ALU.mult, op1=ALU.mult), (rpj, rg, rrs), (rd[c],))
            bfree(bi)

        def st1_full():
            pj, rpj, bi = st["p"]
            pq, rpq, qi = balloc()
            sc.op("pe", lambda e: e.matmul(pq[:], bo64, sq[:], start=True, stop=True), (rsq, r_cm), (rpq,))
            sc.op("act", lambda e: e.activation(out=rs[:], in_=pq[:], func=AF.Ln, bias=epsc), (rpq, r_par), (rrs,))
            bfree(qi)
            sc.op("act", lambda e: e.activation(out=rs[:], in_=rs[:], func=AF.Exp, scale=-0.5), (rrs,), (rrs,))
            sc.op("dve", lambda e: e.scalar_tensor_tensor(
                out=qnT, in0=pj[:], scalar=gcol, in1=rs[:], op0=ALU.mult, op1=ALU.mult),
                  (rpj, rg, rrs), (rqn,))
            bfree(bi)

        def st2_full():
            pr, rpr, ri = balloc()
            sc.op("pe", lambda e: e.matmul(pr[:], rmat, qnT, start=True, stop=True), (r_cm, rqn), (rpr,))
            sc.op("dve", lambda e: e.tensor_tensor(out=t1T[0], in0=pr[:], in1=tE[:, cs], op=ALU.mult),
                  (rpr, r_E[c]), (r_G2[0],))
            bfree(ri)
            sc.op("dve", lambda e: e.tensor_tensor(out=qnT, in0=qnT, in1=tD[:, cs], op=ALU.mult),
                  (rqn, r_D[c]), (rqn,))
            for (dt_, lo, hi, rd) in dests:
                sc.op("dve", lambda e, dt_=dt_, lo=lo, hi=hi: e.tensor_tensor(
                    out=dt_[lo:hi, cs], in0=qnT[lo:hi, :], in1=t1T[0][lo:hi, :], op=ALU.add),
                      (rqn, r_G2[0]), (rd[c],))

        if rot and len(dests) > 1:
            return [st0, st1_full, st2_full]

        def st2():
            pr, rpr, ri = balloc()
            for i, (dt_, lo, hi, rd) in enumerate(dests):
                sc.op("pe", lambda e, dt_=dt_, i=i: e.matmul(pr[:], rmat, dt_[:, cs], start=(i == 0),
                                                             stop=(i == len(dests) - 1)),
                      (r_cm, rd[c]), (rpr,))
            sc.op("dve", lambda e: e.tensor_tensor(out=t1, in0=pr[:], in1=tE[:, cs], op=ALU.mult),
                  (rpr, r_E[c]), (rt1,))
            bfree(ri)
            for (dt_, lo, hi, rd) in dests:
                sc.op("dve", lambda e, dt_=dt_, lo=lo, hi=hi: e.tensor_tensor(
                    out=dt_[lo:hi, cs], in0=dt_[lo:hi, cs], in1=tD[lo:hi, cs], op=ALU.mult),
                      (rd[c], r_D[c]), (rd[c],))
                sc.op("dve", lambda e, dt_=dt_, lo=lo, hi=hi: e.tensor_tensor(
                    out=dt_[lo:hi, cs], in0=dt_[lo:hi, cs], in1=t1[lo:hi, :], op=ALU.add),
                      (rd[c], rt1), (rd[c],))

        return [st0, st1, st2] if rot else [st0, st1]

    def epilogue(hp, c, sa, sb_, rA, rB, on_dve=False):
        cs = slice(c * 512, (c + 1) * 512)
        if on_dve:
            sc.op("dve", lambda e: e.reciprocal(out=G[0][0:64, :], in_=sa[64:128, :]), rA, (r_G[0],))
            sc.op("dve", lambda e: e.reciprocal(out=G[0][64:128, :], in_=sb_[0:64, :]), rB, (r_G[0],))
        else:
            sc.op("act", lambda e: e.activation(out=G[0][0:64, :], in_=sa[64:128, :], func=AF.Ln), rA, (r_G[0],))
            sc.op("act", lambda e: e.activation(out=G[0][64:128, :], in_=sb_[0:64, :], func=AF.Ln), rB, (r_G[0],))
            sc.op("act", lambda e: e.activation(out=G[0][:], in_=G[0][:], func=AF.Exp, scale=-1.0),
                  (r_G[0],), (r_G[0],))
        sc.op("dve", lambda e: e.tensor_tensor(out=G[2][0:64, :], in0=sa[0:64, :], in1=G[0][0:64, :], op=ALU.mult),
              tuple(rA) + (r_G[0],), tuple(r_G2))
        sc.op("dve", lambda e: e.tensor_tensor(out=G[2][64:128, :], in0=sb_[64:128, :], in1=G[0][64:128, :], op=ALU.mult),
              tuple(rB) + (r_G[0],), tuple(r_G2))
        sc.op("dve", lambda e: e.tensor_tensor(out=ogT[:, hp, cs], in0=G[2][:], in1=ogT[:, hp, cs], op=ALU.mult),
              tuple(r_G2) + (r_og[hp][c],), (r_og[hp][c],))

    def out_proj(l):
        li = layers.index(l)
        nxt = layers[li + 1] if li + 1 < len(layers) else None
        ws = [load_w(wout_d[l, nh, :, :]) for nh in range(2)]
        if nxt is not None:
            rmsnorm_prep(nxt)
            if nxt % 2 == 1:
                pos_prefetch()

        def out_block(tb):
            for nh in range(2):
                wt, rw = ws[nh]
                pj, rpj, bi = balloc()
                for ic in range(8):
                    sc.op("pe", lambda e, ic=ic, pj=pj, wt=wt: e.matmul(
                        pj[:], ogT[:, ic, tb * 128:(tb + 1) * 128], wt[:, ic * 512:(ic + 1) * 512],
                        start=(ic == 0), stop=(ic == 7)),
                          (rw, r_og[ic][tb // 4]), (rpj,))
                sc.op("dve", lambda e, nh=nh, pj=pj: e.tensor_tensor(
                    out=xres[:, tb, nh * 512:(nh + 1) * 512], in0=pj[:],
                    in1=xres[:, tb, nh * 512:(nh + 1) * 512], op=ALU.add),
                      (rpj, r_x[tb]), (r_x[tb],))
                bfree(bi)

        jobs = []
        for tb in range(16):
            j_ = [lambda tb=tb: out_block(tb)]
            if nxt is not None:
                j_ += [(lambda tb=tb: rms_a(nxt, tb)), (lambda tb=tb: rms_b(nxt, tb))]
            jobs.append(j_)
        run_pipeline(jobs)

    def fox_layer(l):
        j = l // 2
        pb = l * PW
        rmsnorm_to_hT(l)
        sc.op("dve", lambda e: e.memset(tC[64:128, :], 0.0), (), tuple(r_C))
        sc.op("dve", lambda e: e.memset(tD[0:64, :], 0.0), (), tuple(r_D))
        sc.op("dve", lambda e: e.memset(tA[64:128, :], 0.0), (), tuple(r_A))
        sc.op("dve", lambda e: e.memset(tB[0:64, :], 0.0), (), tuple(r_B))
        sc.op("dve", lambda e: e.tensor_scalar(out=sm[:, 32:33], in0=par[:, pb + 8:pb + 9], scalar1=0.125,
                                               scalar2=None, op0=ALU.mult), (r_par,), (r_sm,))
        wt, rw = load_w(wff_d[j, :, :], 768)
        for c in range(4):
            pj, rpj, bi = balloc()
            proj_mm(wt, rw, 0, c, pj, rpj, M=96, wcols=96)
            cs = slice(c * 512, (c + 1) * 512)
            sc.op("dve", lambda e, pj=pj, cs=cs: e.tensor_scalar(out=f1[0:96, cs], in0=pj[0:96, :],
                                                                 scalar1=par[0:96, pb + 10:pb + 11], scalar2=None,
                                                                 op0=ALU.add), (rpj, r_par), (r_f1[c],))
            bfree(bi)
            sc.op("act", lambda e, cs=cs: e.activation(out=f1[0:96, cs], in_=f1[0:96, cs], func=AF.Exp, scale=-1.0),
                  (r_f1[c],), (r_f1[c],))
            sc.op("act", lambda e, cs=cs: e.activation(out=f1[0:96, cs], in_=f1[0:96, cs], func=AF.Ln, bias=1.0),
                  (r_f1[c],), (r_f1[c],))
        for c in range(4):
            cs = slice(c * 512, (c + 1) * 512)
            init = 0.0 if c == 0 else f1[0:96, c * 512 - 1:c * 512]
            sc.op("dve", lambda e, cs=cs, init=init: e.tensor_tensor_scan(
                out=f1[0:96, cs], data0=onec[0:96, :].to_broadcast([96, 512]),
                data1=f1[0:96, cs], initial=init, op0=ALU.mult, op1=ALU.add),
                  (r_f1[c], r_par) + ((r_f1[c - 1],) if c else ()), (r_f1[c],))
        sc.op("dve", lambda e: e.memset(tE[96:97, :], 1.0), (), tuple(r_E))
        for c in range(4):
            cs = slice(c * 512, (c + 1) * 512)
            sc.op("dve", lambda e, cs=cs: e.tensor_copy(out=H[0][0:96, :], in_=f1[0:96, cs]), (r_f1[c],), (r_H[0],))
            sc.op("dve", lambda e, cs=cs: e.tensor_tensor(out=G[0][0:96, :], in0=f1[0:96, cs], in1=H[0][0:96, :],
                                                          op=ALU.subtract), (r_f1[c], r_H[0]), (r_G[0],))
            sc.op("dve", lambda e: e.tensor_copy(out=H[1][0:96, :], in_=G[0][0:96, :]), (r_G[0],), (r_H[1],))
            sc.op("dve", lambda e: e.tensor_tensor(out=G[1][0:96, :], in0=G[0][0:96, :], in1=H[1][0:96, :],
                                                   op=ALU.subtract), (r_G[0], r_H[1]), (r_G[1],))
            sc.op("dve", lambda e, cs=cs: e.tensor_copy(out=tE[0:16, cs], in_=H[0][0:16, :]), (r_H[0],), (r_E[c],))
            sc.op("dve", lambda e, cs=cs: e.tensor_copy(out=tE[32:48, cs], in_=H[1][32:48, :]), (r_H[1],), (r_E[c],))
            sc.op("dve", lambda e, cs=cs: e.tensor_copy(out=tE[64:80, cs], in_=G[1][64:80, :]), (r_G[1],), (r_E[c],))


        def aug_job(hp, side, c):
            sel = selm[:, side * 8 + hp, :]
            dA, dB = (tA, tB) if side == 0 else (tC, tD)
            rdA, rdB = (r_A, r_B) if side == 0 else (r_C, r_D)
            cs = slice(c * 512, (c + 1) * 512)

            def st0():
                pj, rpj, bi = balloc()
                sc.op("pe", lambda e: e.matmul(pj[0:70, :], sel, tE[:, cs], start=True, stop=True), (r_cm, r_E[c]), (rpj,))
                sc.op("dve", lambda e: e.tensor_copy(out=dA[64:70, cs], in_=pj[64:70, :]), (rpj,), (rdA[c],))
                sc.op("dve", lambda e: e.tensor_copy(out=dB[0:6, cs], in_=pj[0:6, :]), (rpj,), (rdB[c],))
                bfree(bi)
            return [st0]

        cidx = 0
        for hp in range(8):
            wt, rw = load_w(wfox_d[j, hp, :, :])
            jobs = [gate_job(wt, rw, c, hp) for c in range(4)]
            for side in (0, 1):
                for c in range(4):
                    jobs.append(aug_job(hp, side, c))
            for c in range(4):
                jobs.append(qk_job(wt, rw, 0, c, sm[:, 32:33], r_sm,
                                   [(tA, 0, 64, r_A), (tB, 64, 128, r_B)], False))
            for c in range(4):
                jobs.append(qk_job(wt, rw, 1, c, par[:, pb + 9:pb + 10], r_par,
                                   [(tC, 0, 64, r_C), (tD, 64, 128, r_D)], False))
            tks = [slice(b * 128, (b + 1) * 128) for b in range(16)]
            for g4 in range(4):
                jobs.append(v_job(wt, rw, tks, g4))
            run_pipeline(jobs)
            ul = [(balloc(), balloc()) for _ in range(2)]
            units = [(c, kb) for c in range(4) for kb in range(4 * c + 4)]
            pendq = []
            for ui in range(len(units) + 2):
                if ui < len(units):
                    c, kb = units[ui]
                    w = 512 if kb < 4 * c else (4 * c + 4 - kb) * 128
                    q0 = 512 * (c + 1) - w
                    sa, rsa, sai = balloc()
                    sbk, rsb, sbi = balloc()
                    pa, pb_ = ptall[:, (2 * ui) % 6, :], ptall[:, (2 * ui + 1) % 6, :]
                    rpa, rpb = r_pt[(2 * ui) % 6], r_pt[(2 * ui + 1) % 6]
                    ks = slice(kb * 128, (kb + 1) * 128)
                    qs = slice(q0, q0 + w)
                    qch = chunks_of(q0, q0 + w)
                    diag = kb >= 4 * c
                    for (st_, mv_, rst, rmv, sbank, rsbank) in ((tC, tA, r_C, r_A, sa, rsa), (tD, tB, r_D, r_B, sbk, rsb)):
                        sc.op("pe", lambda e, st_=st_, mv_=mv_, sbank=sbank, ks=ks, qs=qs, w=w, diag=diag: e.matmul(
                            sbank[:, 0:w], st_[:, ks], mv_[:, qs], start=True, stop=not diag),
                              (rst[kb // 4],) + tuple(rmv[i] for i in qch), (rsbank,))
                        if diag:
                            sc.op("pe", lambda e, sbank=sbank: e.matmul(sbank[:, 0:128], ident, mcur,
                                                                        start=False, stop=True),
                                  (r_cm,), (rsbank,))
                    sc.op("act", lambda e, pa=pa, sa=sa, w=w: e.activation(out=pa[:, 0:w], in_=sa[:, 0:w], func=AF.Exp),
                          (rsa,), (rpa,))
                    sc.op("act", lambda e, pb_=pb_, sbk=sbk, w=w: e.activation(out=pb_[:, 0:w], in_=sbk[:, 0:w], func=AF.Exp),
                          (rsb,), (rpb,))
                    bfree(sai)
                    bfree(sbi)
                    pendq.append((c, kb, w, pa, pb_, rpa, rpb))
                if ui >= 2 and pendq:
                    c_, kb_, w_, pa_, pbb_, rpa_, rpb_ = pendq.pop(0)
                    (UB, r_UB, _u), (LB, r_LB, _l) = ul[c_ % 2]
                    nkb = 4 * c_ + 4
                    first = kb_ == 0
                    last = kb_ == nkb - 1
                    osl = slice(512 - w_, 512)
                    rl = r_v[kb_ // 4]
                    sc.op("pe", lambda e, UB=UB, kb_=kb_, pa_=pa_, osl=osl, w_=w_, first=first, last=last: e.matmul(
                        UB[:, osl], vt[:, kb_, 0:128], pa_[:, 0:w_], start=first, stop=last, skip_group_check=True),
                          (rl, rpa_), (r_UB,))
                    sc.op("pe", lambda e, LB=LB, kb_=kb_, pbb_=pbb_, osl=osl, w_=w_, first=first, last=last: e.matmul(
                        LB[:, osl], vt[:, kb_, 64:192], pbb_[:, 0:w_], start=first, stop=last, skip_group_check=True),
                          (rl, rpb_), (r_LB,))
                    if last:
                        epilogue(hp, c_, UB, LB, (r_UB,), (r_LB,), on_dve=True)
            for (a_, b_) in ul:
                bfree(a_[2])
                bfree(b_[2])
        out_proj(l)

    def pos_prefetch():
        sc.dma("sp", "posall", lambda e: e.dma_start(out=f1[:].bitcast(I32), in_=pos_d[:, :]), (), tuple(r_f1))

    def rotary_tables():
        for c in range(4):
            cs = slice(c * 512, (c + 1) * 512)
            gi = G[0][:].bitcast(I32)
            sc.op("dve", lambda e, cs=cs: e.tensor_copy(out=G[1][:], in_=f1[:].bitcast(I32)[:, cs]),
                  (r_f1[c],), (r_G[1],))
            sc.op("dve", lambda e: e.tensor_scalar(out=G[2][:], in0=G[1][:], scalar1=cpar[:, 0:1], scalar2=None,
                                                   op0=ALU.mult), (r_G[1], r_par), tuple(r_G2))
            sc.op("dve", lambda e, gi=gi: e.tensor_copy(out=gi, in_=G[2][:]), tuple(r_G2), (r_G[0],))
            sc.op("dve", lambda e, gi=gi: e.tensor_copy(out=f2[:, 0:512], in_=gi), (r_G[0],), (r_f2[0],))
            sc.op("dve", lambda e: e.tensor_tensor(out=G[2][:], in0=G[2][:], in1=f2[:, 0:512], op=ALU.subtract),
                  tuple(r_G2) + (r_f2[0],), tuple(r_G2))
            sc.op("act", lambda e, cs=cs: e.activation(out=tE[:, cs], in_=G[2][:], func=AF.Sin, scale=cpar[:, 1:2]),
                  tuple(r_G2) + (r_par,), (r_E[c],))
            sc.op("dve", lambda e: e.scalar_tensor_tensor(out=G[1][:], in0=G[2][:], scalar=-1.0, in1=G[2][:],
                                                          op0=ALU.mult, op1=ALU.max), tuple(r_G2), (r_G[1],))
            sc.op("act", lambda e, cs=cs: e.activation(out=tD[:, cs], in_=G[1][:], func=AF.Sin,
                                                       scale=float(-2 * np.pi), bias=float(np.pi / 2)),
                  (r_G[1],), (r_D[c],))

    def dsw_layer(l):
        j = l // 2
        pb = l * PW
        rmsnorm_to_hT(l)
        if l == layers[0]:
            pos_prefetch()
        rotary_tables()
        sc.op("dve", lambda e: e.memset(tA[64:128, :], 0.0), (), tuple(r_A))
        sc.op("dve", lambda e: e.memset(tB[0:64, :], 0.0), (), tuple(r_B))
        sc.op("dve", lambda e: e.tensor_scalar(out=sm[:, 33:36], in0=par[:, pb + 8:pb + 11], scalar1=0.125,
                                               scalar2=None, op0=ALU.mult), (r_par,), (r_sm,))
        for i_, m_ in enumerate((3, 3, 4, 4)):
            sc.op("dve", lambda e, i_=i_, m_=m_: e.tensor_copy(out=ptall[:, 3, i_ * 128:(i_ + 1) * 128], in_=cmat[:, m_, :]),
                  (r_cm,), (r_pt[3],))
        qidx = 0
        for hp in range(8):
            for g, (window, r) in enumerate(DSWA):
                nb = 16 // r
                wt, rw = load_w(wdsw_d[j, hp * 3 + g, :, :])

                def tsl(rho, jb, r=r):
                    st0 = rho + r * 128 * jb
                    return slice(st0, st0 + 127 * r + 1, r) if r > 1 else slice(st0, st0 + 128)

                blks = [(rho, jb) for rho in range(r) for jb in range(nb)]
                jobs = []
                if g == 0:
                    jobs += [gate_job(wt, rw, c, hp) for c in range(4)]
                for c in range(4):
                    jobs.append(qk_job(wt, rw, 0, c, sm[:, 33 + g:34 + g], r_sm,
                                       [(tA, 0, 64, r_A), (tB, 64, 128, r_B)], True))
                for c in range(4):
                    jobs.append(qk_job(wt, rw, 1, c, par[:, pb + 11 + g:pb + 12 + g], r_par,
                                       [(tC, 0, 128, r_C)], True))
                tks = [tsl(rho, jb) for (rho, jb) in blks]
                for g4 in range(4):
                    jobs.append(v_job(wt, rw, tks, g4))
                run_pipeline(jobs)

                ul = [(balloc(), balloc()) for _ in range(2)]
                ajobs = []
                for bi_ in range(16):
                    def mk(bi_=bi_, qd0=qidx):
                        qd, s4 = bi_ // 4, bi_ % 4
                        rho, jb = blks[bi_]
                        keys = [bi_] + ([bi_ - 1] if jb > 0 else [])
                        ncol = 256 * len(keys)
                        pt, rpt = ptall[:, bi_ % 3, :], r_pt[bi_ % 3]
                        qsl = tsl(rho, jb)
                        (UB, r_UB, _u), (LB, r_LB, _l) = ul[(qd0 + qd) % 2]

                        def st0():
                            sbank, rsbank, si = balloc()
                            first_mm = True
                            for ki, kblk in enumerate(keys):
                                ksl = tsl(*blks[kblk])
                                for h, (mv_, rmv) in enumerate(((tA, r_A), (tB, r_B))):
                                    col = (2 * ki + h) * 128
                                    sc.op("pe", lambda e, col=col, ksl=ksl, mv_=mv_, fm=first_mm: e.matmul(
                                        sbank[:, col:col + 128], tC[:, ksl], mv_[:, qsl], start=fm, stop=False,
                                        skip_group_check=True),
                                          tuple(r_C) + tuple(rmv), (rsbank,))
                                    first_mm = False
                            sc.op("pe", lambda e: e.matmul(sbank[:, 0:ncol], ident, ptall[:, 3, 0:ncol], start=False,
                                                           stop=True, skip_group_check=True),
                                  (r_cm, r_pt[3]), (rsbank,))
                            sc.op("act", lambda e: e.activation(out=pt[:, 0:ncol], in_=sbank[:, 0:ncol], func=AF.Exp),
                                  (rsbank,), (rpt,))
                            bfree(si)

                        def st1():
                            for ki, kblk in enumerate(keys):
                                rl = r_v[kblk // 4]
                                fs = (s4 == 0 and ki == 0)
                                sc.op("pe", lambda e, ki=ki, kblk=kblk, fs=fs: e.matmul(
                                    UB[:, s4 * 128:(s4 + 1) * 128], vt[:, kblk, 0:128], pt[:, (2 * ki) * 128:(2 * ki + 1) * 128],
                                    start=fs, stop=False, skip_group_check=True), (rl, rpt), (r_UB,))
                                sc.op("pe", lambda e, ki=ki, kblk=kblk, fs=fs: e.matmul(
                                    LB[:, s4 * 128:(s4 + 1) * 128], vt[:, kblk, 64:192], pt[:, (2 * ki + 1) * 128:(2 * ki + 2) * 128],
                                    start=fs, stop=False, skip_group_check=True), (rl, rpt), (r_LB,))
                            if s4 == 3:
                                if r == 1:
                                    av = lambda t: t[:, qd * 512:(qd + 1) * 512]
                                    rch = (qd,)
                                    pv = lambda b: b[:]
                                elif r == 4:
                                    av = lambda t: t[:, qd:S:4]
                                    rch = (0, 1, 2, 3)
                                    pv = lambda b: b[:]
                                else:
                                    av = lambda t: t[:].rearrange("p (i r) -> p r i", r=16)[:, 4 * qd:4 * qd + 4, :]
                                    rch = (0, 1, 2, 3)
                                    pv = lambda b: b[:].rearrange("p (s i) -> p s i", s=4)
                                for (bank, rbank, acc, racc) in ((UB, r_UB, f1, r_f1), (LB, r_LB, f2, r_f2)):
                                    if g == 0:
                                        sc.op("act", lambda e, bank=bank, acc=acc: e.copy(out=av(acc), in_=pv(bank)),
                                              (rbank,), tuple(racc[i] for i in rch))
                                    else:
                                        sc.op("dve", lambda e, bank=bank, acc=acc: e.tensor_tensor(
                                            out=av(acc), in0=pv(bank), in1=av(acc), op=ALU.add),
                                              (rbank,) + tuple(racc[i] for i in rch), tuple(racc[i] for i in rch))
                        return [st0, (lambda: None), st1]
                    ajobs.append(mk())
                run_pipeline(ajobs)
                for (a_, b_) in ul:
                    bfree(a_[2])
                    bfree(b_[2])
                qidx += 4
            for c in range(4):
                cs = slice(c * 512, (c + 1) * 512)
                epilogue(hp, c, f1[:, cs], f2[:, cs], (r_f1[c],), (r_f2[c],))
        out_proj(l)

    for l in layers:
        if l % 2 == 0:
            fox_layer(l)
        else:
            dsw_layer(l)

    for b in range(16):
        sc.dma("sp", "o%d" % b,
               lambda e, b=b: e.dma_start(out=y_d[b * 128:(b + 1) * 128, :], in_=xres[:, b, :]),
               (r_x[b],), ())
    outres = []
    for b in range(16):
        rr = Res("oo%d" % b)
        rr.w = ("d", "o%d" % b, 1)
        outres.append(rr)
    sc.wait_all("sp", outres)
    sc.emit(nc, es)
    es.close()
    return nc


def _consts():
    cm = np.zeros((6, 128, 128), np.float32)
    cm[0] = np.eye(128)
    p = np.arange(128)
    cm[1] = (p[:, None] // 64 == p[None, :] // 64) / 64.0
    for base in (0, 64):
        for i in range(8):
            cm[2][base + i + 8, base + i] = 1.0
            cm[2][base + i, base + i + 8] = 1.0
    k = p[:, None]
    q = p[None, :]
    cm[3] = np.where(q >= k, 0.0, NEG)
    cm[4] = np.where(q <= k, 0.0, NEG)
    sel = np.zeros((16, 128, 128), np.float32)
    for hp in range(8):
        a, b = 2 * hp, 2 * hp + 1
        sq = sel[hp]
        sk = sel[8 + hp]
        for jj in range(3):
            sq[32 * jj + a, 64 + jj] = -1.0
            sq[96, 64 + 3 + jj] = 1.0
            sq[32 * jj + b, jj] = -1.0
            sq[96, 3 + jj] = 1.0
            sk[96, 64 + jj] = 1.0
            sk[32 * jj + a, 64 + 3 + jj] = 1.0
            sk[96, jj] = 1.0
            sk[32 * jj + b, 3 + jj] = 1.0
    cmat = np.ascontiguousarray(np.concatenate(
        [cm.transpose(1, 0, 2).reshape(128, 6 * 128), sel[:, :, 0:70].transpose(1, 0, 2).reshape(128, 16 * 70)], axis=1))
    inv_freq = (np.float32(500000.0) ** (-np.arange(0, 16, 2, dtype=np.float32) / np.float32(16))).astype(np.float32)
    cpar = np.zeros((128, 4), np.float32)
    for base in (0, 64):
        for i in range(8):
            cpar[base + i, 0] = inv_freq[i] / (2 * np.pi)
            cpar[base + i + 8, 0] = inv_freq[i] / (2 * np.pi)
            cpar[base + i, 1] = -2 * np.pi
            cpar[base + i + 8, 1] = 2 * np.pi
    return cmat, cpar


def _prep_shared(norm_g, fox_w_in, fox_b_f, fox_q_norm, fox_k_norm, fox_w_out,
                 dsw_w_in, dsw_q_norm, dsw_k_norm, dsw_w_out):
    f = np.float32
    par = np.zeros((128, DEPTH * PW), f)
    p = np.arange(128)
    for l in range(DEPTH):
        par[:, l * PW:l * PW + 8] = np.asarray(norm_g[l], f).reshape(8, 128).T
        j = l // 2
        if l % 2 == 0:
            par[:, l * PW + 8] = np.asarray(fox_q_norm[j], f)[p % 64]
            par[:, l * PW + 9] = np.asarray(fox_k_norm[j], f)[p % 64]
            for base in (0, 32, 64):
                par[base:base + 16, l * PW + 10] = np.asarray(fox_b_f[j], f)
        else:
            for g in range(3):
                par[:, l * PW + 8 + g] = np.asarray(dsw_q_norm[j, g], f)[p % 64]
                par[:, l * PW + 11 + g] = np.asarray(dsw_k_norm[j, g], f)[p % 64]
    par[:, DEPTH * PW - 1] = EPS
    par[:, DEPTH * PW - 2] = 1.0
    w = np.asarray(fox_w_in, f)
    w4 = w[:, :, :4096].reshape(2, 8, 128, 4, 8, 128)
    wfox = np.ascontiguousarray(w4.transpose(0, 4, 2, 1, 3, 5)).reshape(2, 8, 128, 4096)
    wf = w[:, :, 4096:4112].reshape(2, 8, 128, 16)
    wff = np.zeros((2, 128, 8, 96), f)
    for base in (0, 32, 64):
        wff[:, :, :, base:base + 16] = wf.transpose(0, 2, 1, 3)
    wff = wff.reshape(2, 128, 768)
    wd = np.asarray(dsw_w_in, f).reshape(2, 8, 128, 10240)
    wdsw = np.empty((2, 8, 3, 128, 8, 4, 128), f)
    for hp in range(8):
        for g in range(3):
            for u in range(3):
                c0 = u * 3072 + g * 1024 + hp * 128
                wdsw[:, hp, g, :, :, u, :] = wd[:, :, :, c0:c0 + 128].transpose(0, 2, 1, 3)
            c0 = 9216 + hp * 128
            wdsw[:, hp, g, :, :, 3, :] = wd[:, :, :, c0:c0 + 128].transpose(0, 2, 1, 3)
    wdsw = wdsw.reshape(2, 24, 128, 4096)
    wout = np.empty((4, 2, 128, 8, 512), f)
    for l in range(DEPTH):
        wo = np.asarray(fox_w_out[l // 2] if l % 2 == 0 else dsw_w_out[l // 2], f).reshape(8, 128, 2, 512)
        wout[l] = wo.transpose(2, 1, 0, 3)
    wout = wout.reshape(4, 2, 128, 4096)
    cmat, cpar = _consts()
    return {"par": par, "cpar": cpar, "cmat": cmat, "wfox": wfox, "wff": wff, "wdsw": wdsw, "wout": wout}


_PROG_CACHE = {}


def _get_prog(layers):
    key = tuple(layers)
    if key not in _PROG_CACHE:
        _PROG_CACHE[key] = build_program(list(layers))
    return _PROG_CACHE[key]


def kernel(x, positions, norm_g, fox_w_in, fox_b_f, fox_q_norm, fox_k_norm, fox_w_out,
           dsw_w_in, dsw_q_norm, dsw_k_norm, dsw_w_out, _groups=None):
    shared = _prep_shared(norm_g, fox_w_in, fox_b_f, fox_q_norm, fox_k_norm, fox_w_out,
                          dsw_w_in, dsw_q_norm, dsw_k_norm, dsw_w_out)
    x = np.asarray(x, np.float32)
    positions = np.asarray(positions, np.int32)
    B = x.shape[0]
    cur = [np.ascontiguousarray(x[b]) for b in range(B)]
    posb = [np.ascontiguousarray(np.broadcast_to(positions[b][None, :], (128, S))) for b in range(B)]
    for layers in (_groups or LAUNCH_GROUPS):
        nc = _get_prog(layers)
        in_maps = []
        for b in range(B):
            m = dict(shared)
            m["x"] = cur[b]
            m["pos"] = posb[b]
            in_maps.append(m)
        res = run_bass_kernel_spmd(nc, in_maps, core_ids=list(range(B)))
        cur = [np.asarray(res.results[b]["y"], np.float32) for b in range(B)]
    return np.stack(cur, axis=0)
```
